# Optimizing a Trainium2 kernel written in Bass

```python
import math
import jax
import jax.numpy as jnp
from jax import lax
import numpy as np

D_MODEL = 1024
BATCH = 2
SEQ = 16384
DEPTH = 2

D_MIX = D_MODEL
HG_HEADS = 4
HG_DK = 64
HG_DV = 64
HG_QK = HG_HEADS * HG_DK
HG_W = HG_HEADS * HG_DV
GD_HEADS = 4
GD_DK = 128
GD_DV = 128
GD_QK = GD_HEADS * GD_DK
GD_W = GD_HEADS * GD_DV
RG_W = D_MIX - HG_W - GD_W
RG_BLOCKS = 4
RG_BD = RG_W // RG_BLOCKS
RG_C = 8.0
CONV_W = 4
D_FF = 2816
CHUNK = 64
EPS = 1e-6
SPLIT_SIZES = (HG_QK, HG_QK, HG_W, HG_W, GD_QK, GD_QK, GD_W, GD_W, GD_HEADS, GD_HEADS, RG_W, RG_W)
SPLIT_POINTS = tuple(int(p) for p in np.cumsum(SPLIT_SIZES)[:-1])
N_IN = int(sum(SPLIT_SIZES))

kernel_name = 'hybrid_hgrn2_gdn_rglru_macaron'


def rmsnorm(x, w):
    xf = x.astype(jnp.float32)
    y = xf * lax.rsqrt(jnp.mean(xf * xf, axis=-1, keepdims=True) + EPS)
    return (y * w.astype(jnp.float32)).astype(x.dtype)


def head_rmsnorm(o, w):
    return o * lax.rsqrt(jnp.mean(o * o, axis=-1, keepdims=True) + EPS) * w.astype(jnp.float32)


def l2norm(t):
    return t * lax.rsqrt(jnp.sum(t * t, axis=-1, keepdims=True) + EPS)


def causal_depthwise_conv(x, w):
    return lax.conv_general_dilated(
        x, w[:, None, :].astype(x.dtype), window_strides=(1,), padding=[(CONV_W - 1, 0)],
        dimension_numbers=('NWC', 'WIO', 'NWC'), feature_group_count=x.shape[-1])


def to_chunks(t):
    b, l, h, d = t.shape
    return t.reshape(b, l // CHUNK, CHUNK, h, d).transpose(1, 0, 3, 2, 4)


def from_chunks(t):
    n, b, h, c, d = t.shape
    return t.transpose(1, 0, 3, 2, 4).reshape(b, n * c, h, d)


def swiglu(h, w_gate, w_up, w_down):
    return (jax.nn.silu(h @ w_gate) * (h @ w_up)) @ w_down


def hgrn2_chunked(q, k, v, logf):
    qc, kc, vc, gc = to_chunks(q), to_chunks(k), to_chunks(v), to_chunks(logf)
    _, b, h, _, dk = qc.shape
    dv = vc.shape[-1]
    causal = jnp.tril(jnp.ones((CHUNK, CHUNK), dtype=bool))[:, :, None]

    def step(state, inp):
        qi, ki, vi, gi = inp
        cum = jnp.cumsum(gi, axis=2)
        diff = cum[:, :, :, None, :] - cum[:, :, None, :, :]
        decay = jnp.exp(jnp.where(causal, diff, -jnp.inf))
        scores = jnp.einsum('bhtd,bhsd,bhtsd->bhts', qi, ki, decay)
        o = (jnp.einsum('bhts,bhsv->bhtv', scores, vi)
             + jnp.einsum('bhtd,bhdv->bhtv', qi * jnp.exp(cum), state))
        last = cum[:, :, -1:, :]
        state = (state * jnp.exp(last[:, :, 0, :, None])
                 + jnp.einsum('bhsd,bhsv->bhdv', ki * jnp.exp(last - cum), vi))
        return state, o

    s0 = jnp.zeros((b, h, dk, dv), jnp.float32)
    _, o = lax.scan(step, s0, (qc, kc, vc, gc))
    return from_chunks(o)


def gated_delta_chunked(q, k, v, g, beta):
    qc, kc, vc = to_chunks(q), to_chunks(k), to_chunks(v)
    gc = to_chunks(g[..., None])[..., 0]
    bc = to_chunks(beta[..., None])[..., 0]
    _, b, h, _, dk = qc.shape
    dv = vc.shape[-1]
    incl = jnp.tril(jnp.ones((CHUNK, CHUNK), dtype=bool))
    strict = jnp.tril(jnp.ones((CHUNK, CHUNK), dtype=bool), k=-1)
    eye = jnp.eye(CHUNK, dtype=jnp.float32)

    def step(state, inp):
        qi, ki, vi, gi, bi = inp
        cum = jnp.cumsum(gi, axis=-1)
        decay = jnp.exp(jnp.where(incl, cum[..., :, None] - cum[..., None, :], -jnp.inf))
        kb = ki * bi[..., None]
        lower = jnp.where(strict, jnp.einsum('bhtd,bhsd->bhts', kb, ki) * decay, 0.0)
        rhs = jnp.concatenate([vi * bi[..., None], kb * jnp.exp(cum)[..., None]], axis=-1)
        sol = lax.linalg.triangular_solve(eye + lower, rhs, left_side=True, lower=True,
                                          unit_diagonal=True)
        u, w = sol[..., :dv], sol[..., dv:]
        v_new = u - jnp.einsum('bhtd,bhdv->bhtv', w, state)
        scores = jnp.einsum('bhtd,bhsd->bhts', qi, ki) * decay
        o = (jnp.einsum('bhtd,bhdv->bhtv', qi * jnp.exp(cum)[..., None], state)
             + jnp.einsum('bhts,bhsv->bhtv', scores, v_new))
        last = cum[..., -1:]
        state = (state * jnp.exp(last)[..., None]
                 + jnp.einsum('bhsd,bhsv->bhdv', ki * jnp.exp(last - cum)[..., None], v_new))
        return state, o

    s0 = jnp.zeros((b, h, dk, dv), jnp.float32)
    _, o = lax.scan(step, s0, (qc, kc, vc, gc, bc))
    return from_chunks(o)


def rg_lru(xc, wr, br, wi, bi, lam):
    b, l, _ = xc.shape
    xb = xc.reshape(b, l, RG_BLOCKS, RG_BD)
    r = jax.nn.sigmoid(jnp.einsum('blni,nio->blno', xb, wr.astype(jnp.float32)).reshape(b, l, RG_W)
                       + br.astype(jnp.float32))
    ig = jax.nn.sigmoid(jnp.einsum('blni,nio->blno', xb, wi.astype(jnp.float32)).reshape(b, l, RG_W)
                        + bi.astype(jnp.float32))
    log_a = -RG_C * r * jax.nn.softplus(-lam.astype(jnp.float32))
    a = jnp.exp(log_a)
    bx = jnp.sqrt(-jnp.expm1(2.0 * log_a)) * (ig * xc)

    def combine(c1, c2):
        a1, b1 = c1
        a2, b2 = c2
        return a1 * a2, a2 * b1 + b2

    _, hs = lax.associative_scan(combine, (a, bx), axis=1)
    return hs


def hybrid_mixer(h, lb, w_in, hg_norm_w, gd_conv_w, gd_a_log, gd_dt_bias, gd_norm_w,
                 rg_conv_w, rg_conv_b, rg_wr, rg_br, rg_wi, rg_bi, rg_lambda, w_out):
    b, l, _ = h.shape
    f32 = jnp.float32
    proj = (h @ w_in).astype(f32)
    (hg_q, hg_f, hg_i, hg_g, gd_q, gd_k, gd_v, gd_z, gd_b, gd_a,
     rg_x, rg_gate) = jnp.split(proj, SPLIT_POINTS, axis=-1)

    lbf = lb.reshape(HG_HEADS, HG_DK)
    q = jax.nn.silu(hg_q).reshape(b, l, HG_HEADS, HG_DK) * (HG_DK ** -0.5)
    f = lbf + (1.0 - lbf) * jax.nn.sigmoid(hg_f.reshape(b, l, HG_HEADS, HG_DK))
    o_hg = hgrn2_chunked(q, 1.0 - f, hg_i.reshape(b, l, HG_HEADS, HG_DV), jnp.log(f))
    o_hg = head_rmsnorm(o_hg, hg_norm_w) * jax.nn.silu(hg_g.reshape(b, l, HG_HEADS, HG_DV))

    qkv = jax.nn.silu(causal_depthwise_conv(jnp.concatenate([gd_q, gd_k, gd_v], axis=-1), gd_conv_w))
    cq, ck, cv = jnp.split(qkv, (GD_QK, 2 * GD_QK), axis=-1)
    cq = l2norm(cq.reshape(b, l, GD_HEADS, GD_DK)) * (GD_DK ** -0.5)
    ck = l2norm(ck.reshape(b, l, GD_HEADS, GD_DK))
    beta = jax.nn.sigmoid(gd_b)
    g = -jnp.exp(gd_a_log.astype(f32)) * jax.nn.softplus(gd_a + gd_dt_bias.astype(f32))
    o_gd = gated_delta_chunked(cq, ck, cv.reshape(b, l, GD_HEADS, GD_DV), g, beta)
    o_gd = head_rmsnorm(o_gd, gd_norm_w) * jax.nn.silu(gd_z.reshape(b, l, GD_HEADS, GD_DV))

    xc = causal_depthwise_conv(rg_x, rg_conv_w) + rg_conv_b.astype(f32)
    o_rg = rg_lru(xc, rg_wr, rg_br, rg_wi, rg_bi, rg_lambda) * jax.nn.gelu(rg_gate, approximate=True)

    y = jnp.concatenate([o_hg.reshape(b, l, HG_W), o_gd.reshape(b, l, GD_W), o_rg], axis=-1)
    return y.astype(h.dtype) @ w_out


def setup_inputs(seed: int = 0) -> dict:
    key = jax.random.key(seed)
    ks = jax.random.split(key, 32)
    f32 = jnp.float32

    def nrm(k, shape, scale):
        return jax.random.normal(k, shape, f32) * scale

    def gain(k, shape):
        return 1.0 + 0.02 * jax.random.normal(k, shape, f32)

    dt = jnp.exp(jax.random.uniform(ks[12], (DEPTH, GD_HEADS), f32, math.log(1e-3), math.log(1e-1)))
    a0 = jax.random.uniform(ks[19], (DEPTH, RG_W), f32, 0.9, 0.999) ** (1.0 / RG_C)
    return {
        'x': nrm(ks[0], (BATCH, SEQ, D_MODEL), 1.0),
        'norm_ffn1': gain(ks[1], (DEPTH, D_MODEL)),
        'ffn1_gate': nrm(ks[2], (DEPTH, D_MODEL, D_FF), D_MODEL ** -0.5),
        'ffn1_up': nrm(ks[3], (DEPTH, D_MODEL, D_FF), D_MODEL ** -0.5),
        'ffn1_down': nrm(ks[4], (DEPTH, D_FF, D_MODEL), D_FF ** -0.5),
        'norm_mix': gain(ks[5], (DEPTH, D_MODEL)),
        'w_in': nrm(ks[6], (DEPTH, D_MODEL, N_IN), D_MODEL ** -0.5),
        'hg_lb': nrm(ks[7], (DEPTH, HG_QK), 1.0),
        'hg_norm_w': gain(ks[8], (DEPTH, HG_DV)),
        'gd_conv_w': nrm(ks[9], (DEPTH, CONV_W, 2 * GD_QK + GD_W), CONV_W ** -0.5),
        'gd_a_log': jnp.log(jax.random.uniform(ks[10], (DEPTH, GD_HEADS), f32, 1.0, 16.0)),
        'gd_dt_bias': dt + jnp.log(-jnp.expm1(-dt)),
        'gd_norm_w': gain(ks[11], (DEPTH, GD_DV)),
        'rg_conv_w': nrm(ks[13], (DEPTH, CONV_W, RG_W), CONV_W ** -0.5),
        'rg_conv_b': nrm(ks[14], (DEPTH, RG_W), 0.02),
        'rg_wr': nrm(ks[15], (DEPTH, RG_BLOCKS, RG_BD, RG_BD), RG_BD ** -0.5),
        'rg_br': nrm(ks[16], (DEPTH, RG_W), 0.1),
        'rg_wi': nrm(ks[17], (DEPTH, RG_BLOCKS, RG_BD, RG_BD), RG_BD ** -0.5),
        'rg_bi': nrm(ks[18], (DEPTH, RG_W), 0.1),
        'rg_lambda': jnp.log(a0) - jnp.log1p(-a0),
        'w_out': nrm(ks[20], (DEPTH, D_MIX, D_MODEL), D_MIX ** -0.5),
        'norm_ffn2': gain(ks[21], (DEPTH, D_MODEL)),
        'ffn2_gate': nrm(ks[22], (DEPTH, D_MODEL, D_FF), D_MODEL ** -0.5),
        'ffn2_up': nrm(ks[23], (DEPTH, D_MODEL, D_FF), D_MODEL ** -0.5),
        'ffn2_down': nrm(ks[24], (DEPTH, D_FF, D_MODEL), D_FF ** -0.5),
        'norm_final': gain(ks[25], (D_MODEL,)),
    }


def reference(x, norm_ffn1, ffn1_gate, ffn1_up, ffn1_down, norm_mix, w_in, hg_lb, hg_norm_w,
              gd_conv_w, gd_a_log, gd_dt_bias, gd_norm_w, rg_conv_w, rg_conv_b, rg_wr, rg_br,
              rg_wi, rg_bi, rg_lambda, w_out, norm_ffn2, ffn2_gate, ffn2_up, ffn2_down, norm_final):
    lb_all = jnp.cumsum(jax.nn.softmax(hg_lb.astype(jnp.float32), axis=0), axis=0)
    lb_all = lb_all - lb_all[0]
    for layer in range(DEPTH):
        x = x + 0.5 * swiglu(rmsnorm(x, norm_ffn1[layer]), ffn1_gate[layer], ffn1_up[layer],
                             ffn1_down[layer])
        x = x + hybrid_mixer(rmsnorm(x, norm_mix[layer]), lb_all[layer], w_in[layer], hg_norm_w[layer],
                             gd_conv_w[layer], gd_a_log[layer], gd_dt_bias[layer], gd_norm_w[layer],
                             rg_conv_w[layer], rg_conv_b[layer], rg_wr[layer], rg_br[layer],
                             rg_wi[layer], rg_bi[layer], rg_lambda[layer], w_out[layer])
        x = x + 0.5 * swiglu(rmsnorm(x, norm_ffn2[layer]), ffn2_gate[layer], ffn2_up[layer],
                             ffn2_down[layer])
    return rmsnorm(x, norm_final)
```

```python
import numpy as np
from contextlib import ExitStack
import concourse.bass as bass
import concourse.mybir as mybir
from concourse.bass_utils import run_bass_kernel_spmd

F32 = mybir.dt.float32
BF16 = mybir.dt.bfloat16
AF = mybir.ActivationFunctionType
ALU = mybir.AluOpType
AX = mybir.AxisListType

D = 1024
DFF = 2816
NFC = DFF // 128
NKC = D // 128
EPS = 1e-6
NCORES = 8
SEQ = 16384
BATCH = 2
NT_CORE = BATCH * SEQ // NCORES

STRICT_SAME_ENGINE = False


class Buf:
    __slots__ = ("name", "w", "r", "dsem", "excl")

    def __init__(self, name, excl=False):
        self.name = name
        self.w = None
        self.r = []
        self.dsem = None
        self.excl = excl


class Eng:
    def __init__(self, name):
        self.name = name
        self.n = 0
        self.seen = {}
        self.ops = []
        self.pending = []


class Sched:
    ENGS = ("tensor", "vector", "scalar", "gpsimd", "sync")

    def __init__(self, nc):
        self.nc = nc
        self.engs = {e: Eng(e) for e in self.ENGS}
        self.dma_count = {}
        self.sem_names = ["E_" + e for e in self.ENGS]
        self.nbuf = 0

    def buf(self, name):
        self.nbuf += 1
        return Buf(f"{name}_{self.nbuf}")

    def _deps(self, E, reads, writes):
        deps = {}
        own = "E_" + E.name
        raw_self = E.name in ("vector", "scalar", "gpsimd")

        def add(tok, is_raw):
            if tok is None:
                return
            k, v = tok
            if k == own and not raw_self and not STRICT_SAME_ENGINE:
                return
            if k.startswith("D_"):
                v = self.dma_count[k]
            if deps.get(k, 0) < v:
                deps[k] = v
        for b in reads:
            add(b.w, True)
        for b in writes:
            add(b.w, False)
            for t in b.r:
                add(t, False)
        out = list(E.pending)
        E.pending = []
        for k, v in deps.items():
            if E.seen.get(k, 0) >= v:
                continue
            E.seen[k] = v
            out.append((k, v))
        return out

    def _finish(self, tok, reads, writes):
        for b in writes:
            b.w = tok
            b.r = []
        for b in reads:
            if b not in writes:
                if len(b.r) > 64:
                    best = {}
                    for k, v in b.r:
                        if best.get(k, 0) < v:
                            best[k] = v
                    b.r = list(best.items())
                b.r.append(tok)

    def op(self, eng, fn, reads=(), writes=()):
        E = self.engs[eng]
        ex = [b for b in reads if b.excl and b not in writes]
        if ex:
            writes = list(writes) + ex
        waits = self._deps(E, reads, writes)
        E.n += 1
        tok = ("E_" + eng, E.n)
        E.ops.append((waits, fn, "E_" + eng, 1))
        self._finish(tok, reads, writes)
        return tok

    def dma(self, eng, fn, sb, reads=(), writes=(), inc=16):
        E = self.engs[eng]
        waits = self._deps(E, reads, writes)
        if sb.dsem is None:
            sb.dsem = "D_" + sb.name
        sem = sb.dsem
        if sem not in self.dma_count:
            self.dma_count[sem] = 0
            self.sem_names.append(sem)
        self.dma_count[sem] += inc
        tok = (sem, self.dma_count[sem])
        E.ops.append((waits, fn, sem, inc))
        self._finish(tok, reads, writes)
        return tok

    def barrier(self):
        for e in self.ENGS:
            E = self.engs[e]
            for k, v in self.dma_count.items():
                if v > 0 and E.seen.get(k, 0) < v:
                    E.pending.append((k, v))
                    E.seen[k] = v
            for e2 in self.ENGS:
                n = self.engs[e2].n
                k = "E_" + e2
                if (e2 != e or e in ('vector', 'scalar', 'gpsimd')) and n > 0 and E.seen.get(k, 0) < n:
                    E.pending.append((k, n))
                    E.seen[k] = n

    def finish(self):
        self.barrier()
        for e in self.ENGS:
            E = self.engs[e]
            if E.pending:
                E.ops.append((E.pending, None, None, 0))
                E.pending = []

    def emit(self, st):
        nc = self.nc
        assert len(self.sem_names) < 140, len(self.sem_names)
        sems = {k: st.enter_context(nc.semaphore(k)) for k in self.sem_names}
        block = st.enter_context(nc.Block())

        def replay(E):
            def body(h):
                for waits, fn, sk, inc in E.ops:
                    for k, v in waits:
                        h.wait_ge(sems[k], v)
                    if fn is not None:
                        fn(h).then_inc(sems[sk], inc)
            return body
        for e in self.ENGS:
            E = self.engs[e]
            if E.ops:
                getattr(block, e)(replay(E))


def _prod(xs):
    r = 1
    for x in xs:
        r *= x
    return r


class Ctx:
    ARENA = 204 * 1024

    def __init__(self, nc, st):
        self.nc = nc
        self.st = st
        self.S = Sched(nc)
        self.arena = st.enter_context(nc.sbuf_tensor("arena", [128, self.ARENA // 2], BF16))
        self.off = 0
        self.psum = [st.enter_context(nc.psum_tensor(f"pb{i}", [128, 512], F32))[:, :] for i in range(8)]
        self.pbuf = [Buf(f"pb{i}", excl=True) for i in range(8)]
        self.pnext = 0
        self.prot = list(range(8))

    def reset(self, off=0):
        self.off = off

    def sb(self, name, shape, dt):
        P = shape[0]
        esz = 4 if dt == F32 else 2
        n = _prod(shape[1:])
        nbytes = (n * esz + 31) // 32 * 32
        assert self.off + nbytes <= self.ARENA, (name, self.off, nbytes)
        v = self.arena[0:P, self.off // 2:(self.off + n * esz) // 2]
        self.off += nbytes
        if dt == F32:
            v = v.bitcast(F32)
        if len(shape) == 3:
            v = v.rearrange("p (a b) -> p a b", b=shape[2])
        elif len(shape) == 4:
            v = v.rearrange("p (a b c) -> p a b c", b=shape[2], c=shape[3])
        return v, self.S.buf(name)

    def dbg(self, name, ap, B):
        if not getattr(self, "debug", False):
            return
        d = self.nc.dram_tensor("dbg_" + name, list(ap.shape), ap.dtype, kind="ExternalOutput").ap()
        self.S.dma("gpsimd", lambda h: h.dma_start(out=d, in_=ap), B, reads=[B])

    def pbank(self):
        i = self.prot[self.pnext % len(self.prot)]
        self.pnext = (self.pnext + 1) % len(self.prot)
        return self.psum[i], self.pbuf[i]


def phase_ffn(C, NT, x_in, x_out, wg_d, wu_d, wd_d, cst_d, ident_d,
              emit_xnT=None, final_out=None, nwf_d=None, on_tile=None):
    S = C.S
    C.reset()
    TT = min(512, NT)
    NJ = TT // 128
    wg, Bwg = C.sb("wg", [128, NKC, DFF], BF16)
    wu, Bwu = C.sb("wu", [128, NKC, DFF], BF16)
    wd, Bwd = C.sb("wd", [128, NFC, D], BF16)
    cst, Bcst = C.sb("cst", [128, 8], F32)
    idb, Bidb = C.sb("idb", [128, 128], BF16)
    base_off = C.off
    idf, Bidf = C.sb("idf", [128, 128], F32)
    S.dma("sync", lambda h: h.dma_start(out=cst, in_=cst_d), Bcst, writes=[Bcst])
    S.dma("sync", lambda h: h.dma_start(out=idf, in_=ident_d), Bidf, writes=[Bidf])
    S.op("vector", lambda h: h.tensor_copy(out=idb, in_=idf), reads=[Bidf], writes=[Bidb])
    NST = 4
    stg = [C.sb(f"stg{i}", [128, DFF], F32) for i in range(NST)]
    jobs = []
    for kc in range(NKC):
        jobs.append((wg_d[kc * 128:(kc + 1) * 128, :], wg[:, kc, :], DFF, cst[:, kc:kc + 1], Bwg))
        jobs.append((wu_d[kc * 128:(kc + 1) * 128, :], wu[:, kc, :], DFF, cst[:, kc:kc + 1], Bwu))
    for fc in range(NFC):
        jobs.append((wd_d[fc * 128:(fc + 1) * 128, :], wd[:, fc, :], D, 0.5, Bwd))
    cengs = ["vector", "scalar"]
    for i, (src, dst, n, sc, Bw) in enumerate(jobs):
        sv, Bs = stg[i % NST]
        deng = "sync" if i % 2 == 0 else "gpsimd"
        S.dma(deng, lambda h, sv=sv, src=src, n=n: h.dma_start(out=sv[:, 0:n], in_=src), Bs, writes=[Bs])
        ce = cengs[i % 2]
        if ce == "scalar":
            S.op("scalar", lambda h, sv=sv, dst=dst, n=n, sc=sc: h.activation(out=dst, in_=sv[:, 0:n], func=AF.Copy, scale=sc),
                 reads=[Bs, Bcst], writes=[Bw])
        else:
            S.op(ce, lambda h, sv=sv, dst=dst, n=n, sc=sc: h.tensor_scalar(out=dst, in0=sv[:, 0:n], scalar1=sc, scalar2=None, op0=ALU.mult),
                 reads=[Bs, Bcst], writes=[Bw])
    C.dbg("wg", wg, Bwg)
    C.dbg("wd", wd, Bwd)
    C.dbg("idb", idb, Bidb)
    S.barrier()
    C.reset(base_off)
    xs = [C.sb(f"xs{i}", [128, D], F32) for i in range(2)]
    xr = [C.sb(f"xr{i}", [128, D], F32) for i in range(2)]
    junk, Bjunk = C.sb("junk", [128, D], BF16)
    xnb = [C.sb(f"xnb{i}", [128, D], BF16) for i in range(2)]
    xnT, BxnT = C.sb("xnT", [128, NKC, TT], BF16)
    hT, BhT = C.sb("hT", [128, NFC, TT], BF16)
    sg = [C.sb(f"sg{i}", [128, TT], F32) for i in range(2)]
    stat = [C.sb(f"stat{i}", [128, 4 * NJ], F32) for i in range(2)]
    if emit_xnT is not None:
        xT2 = [C.sb(f"xT2{i}", [128, NKC, 128], BF16) for i in range(2)]
    if final_out is not None:
        nwf, Bnwf = C.sb("nwf", [128, D], F32)
        S.dma("sync", lambda h: h.dma_start(out=nwf, in_=nwf_d.partition_broadcast(128)), Bnwf, writes=[Bnwf])
        fo = [C.sb(f"fo{i}", [128, D], F32) for i in range(2)]

    def rms_to_bf16T(src_ap, Bsrc, st_ap, Bst, xn_ap, Bxn, dstT, BdstT):
        S.op("gpsimd", lambda h: h.memset(st_ap[:, 0:1], 0.0), writes=[Bst])
        S.op("scalar", lambda h: h.activation(out=junk, in_=src_ap, func=AF.Square, accum_out=st_ap[:, 0:1]),
             reads=[Bsrc], writes=[Bjunk, Bst])
        S.op("scalar", lambda h: h.activation(out=st_ap[:, 1:2], in_=st_ap[:, 0:1], func=AF.Sqrt, scale=1.0 / D, bias=EPS),
             reads=[Bst], writes=[Bst])
        S.op("vector", lambda h: h.reciprocal(out=st_ap[:, 2:3], in_=st_ap[:, 1:2]), reads=[Bst], writes=[Bst])
        S.op("scalar", lambda h: h.activation(out=xn_ap, in_=src_ap, func=AF.Copy, scale=st_ap[:, 2:3]),
             reads=[Bsrc, Bst], writes=[Bxn])
        pt, Bpt = C.pbank()
        ptb = pt.bitcast(BF16).rearrange("p (a b) -> p a b", b=128)
        for kc in range(NKC):
            S.op("tensor", lambda h, kc=kc: h.transpose(out=ptb[:, kc, :], in_=xn_ap[:, kc * 128:(kc + 1) * 128], identity=idb),
                 reads=[Bxn, Bidb], writes=[Bpt])
        S.op("vector", lambda h: h.tensor_copy(out=dstT, in_=ptb), reads=[Bpt], writes=[BdstT])

    nit = NT // TT
    xnTs = [(xnT, BxnT)]
    if final_out is None:
        xnTs.append(C.sb("xnT_b", [128, NKC, TT], BF16))

    def prologue(it):
        t0 = it * TT
        stv, Bst = stat[it % 2]
        xT_, BxT_ = xnTs[it % len(xnTs)]
        for j in range(NJ):
            xv, Bx = xs[j % 2]
            S.dma("sync", lambda h, xv=xv, r0=t0 + j * 128: h.dma_start(out=xv, in_=x_in[r0:r0 + 128, :]), Bx, writes=[Bx])
            xnv, Bxn = xnb[j % 2]
            rms_to_bf16T(xv, Bx, stv[:, 4 * j:4 * j + 4], Bst, xnv, Bxn, xT_[:, :, j * 128:(j + 1) * 128], BxT_)

    prologue(0)
    for it in range(nit):
        t0 = it * TT
        stv, Bst = stat[it % 2]
        xT_, BxT_ = xnTs[it % len(xnTs)]
        for fc in range(NFC):
            pg, Bpg = C.pbank()
            for kc in range(NKC):
                S.op("tensor", lambda h, fc=fc, kc=kc, pg=pg, xT_=xT_: h.matmul(
                    pg, lhsT=wg[:, kc, fc * 128:(fc + 1) * 128], rhs=xT_[:, kc, :],
                    start=(kc == 0), stop=(kc == NKC - 1)), reads=[Bwg, BxT_], writes=[Bpg])
            pu, Bpu = C.pbank()
            for kc in range(NKC):
                S.op("tensor", lambda h, fc=fc, kc=kc, pu=pu, xT_=xT_: h.matmul(
                    pu, lhsT=wu[:, kc, fc * 128:(fc + 1) * 128], rhs=xT_[:, kc, :],
                    start=(kc == 0), stop=(kc == NKC - 1)), reads=[Bwu, BxT_], writes=[Bpu])
            sgv, Bsg = sg[fc % 2]
            S.op("scalar", lambda h, pg=pg, sgv=sgv: h.activation(out=sgv, in_=pg, func=AF.Silu),
                 reads=[Bpg], writes=[Bsg])
            S.op("vector", lambda h, pu=pu, sgv=sgv, fc=fc: h.tensor_tensor(
                out=hT[:, fc, :], in0=sgv, in1=pu, op=ALU.mult), reads=[Bpu, Bsg], writes=[BhT])
        if it + 1 < nit:
            prologue(it + 1)
        for j in range(NJ):
            r0 = t0 + j * 128
            ov, Bo = xr[j % 2]
            S.dma("sync", lambda h, ov=ov, r0=r0: h.dma_start(out=ov, in_=x_in[r0:r0 + 128, :]), Bo, writes=[Bo])
            for half in range(2):
                po, Bpo = C.pbank()
                for fc in range(NFC):
                    S.op("tensor", lambda h, fc=fc, j=j, half=half, po=po: h.matmul(
                        po, lhsT=hT[:, fc, j * 128:(j + 1) * 128], rhs=wd[:, fc, half * 512:(half + 1) * 512],
                        start=(fc == 0), stop=(fc == NFC - 1)), reads=[BhT, Bwd], writes=[Bpo])
                S.op("vector", lambda h, ov=ov, half=half, po=po: h.tensor_tensor(
                    out=ov[:, half * 512:(half + 1) * 512], in0=po, in1=ov[:, half * 512:(half + 1) * 512],
                    op=ALU.add), reads=[Bpo, Bo], writes=[Bo])
            if x_out is not None:
                S.dma("sync", lambda h, ov=ov, r0=r0: h.dma_start(out=x_out[r0:r0 + 128, :], in_=ov), Bo, reads=[Bo])
            if emit_xnT is not None or final_out is not None:
                st2 = stv[:, 4 * j:4 * j + 4]
                xnv, Bxn = xnb[j % 2]
                if emit_xnT is not None:
                    tv, Bt = xT2[j % 2]
                    rms_to_bf16T(ov, Bo, st2, Bst, xnv, Bxn, tv, Bt)
                    S.dma("sync", lambda h, tv=tv, r0=r0: h.dma_start(
                        out=(emit_xnT(r0, 128) if callable(emit_xnT) else emit_xnT[:, r0:r0 + 128]).rearrange("(kc p) t -> p kc t", p=128), in_=tv), Bt, reads=[Bt],
                        writes=(emit_xnT.dbuf(r0) if hasattr(emit_xnT, "dbuf") else []))
                else:
                    S.op("gpsimd", lambda h, st2=st2: h.memset(st2[:, 0:1], 0.0), writes=[Bst])
                    S.op("scalar", lambda h, ov=ov, st2=st2: h.activation(out=junk, in_=ov, func=AF.Square, accum_out=st2[:, 0:1]),
                         reads=[Bo], writes=[Bjunk, Bst])
                    S.op("scalar", lambda h, st2=st2: h.activation(out=st2[:, 1:2], in_=st2[:, 0:1], func=AF.Sqrt, scale=1.0 / D, bias=EPS),
                         reads=[Bst], writes=[Bst])
                    S.op("vector", lambda h, st2=st2: h.reciprocal(out=st2[:, 2:3], in_=st2[:, 1:2]), reads=[Bst], writes=[Bst])
                    fv, Bf = fo[j % 2]
                    S.op("vector", lambda h, ov=ov, st2=st2, fv=fv: h.scalar_tensor_tensor(
                        out=fv, in0=ov, scalar=st2[:, 2:3], in1=nwf, op0=ALU.mult, op1=ALU.mult),
                        reads=[Bo, Bst, Bnwf], writes=[Bf])
                    S.dma("sync", lambda h, fv=fv, r0=r0: h.dma_start(out=final_out[r0:r0 + 128, :], in_=fv), Bf, reads=[Bf])
        if on_tile is not None:
            on_tile(it)
    S.barrier()


MIX_NW = 1152
MIX_NCST = 32
C_HQ, C_HF, C_GQ, C_GK, C_GV, C_RX, C_RG, C_GB, C_GA, C_T = 0, 64, 128, 256, 384, 512, 576, 640, 768, 896
SOLVE_DT = BF16
MIX_STOP = 0


def phase_mixer(C, L, xT_src, w_d, cst_d, rows_d, rgw_d, mats_d, seg_d, ident_d, yT_out, use_lb, on_tile=None):
    S = C.S
    C.reset()
    TM = 512
    NCH = 8
    SD = SOLVE_DT

    def V(fn, r=(), w=()):
        S.op("vector", fn, reads=r, writes=w)

    def G(fn, r=(), w=()):
        S.op("gpsimd", fn, reads=r, writes=w)

    def T(fn, r=(), w=()):
        S.op("tensor", fn, reads=r, writes=w)

    def ACT(out, in_, func, r, w, scale=1.0, bias=0.0):
        S.op("scalar", lambda h: h.activation(out=out, in_=in_, func=func, scale=scale, bias=bias), reads=r, writes=w)

    def c3(ap, j=64):
        return ap.rearrange("p (c j) -> p c j", j=j)

    w, Bw = C.sb("w", [128, NKC, MIX_NW], BF16)
    cst, Bcst = C.sb("cst", [128, MIX_NCST], F32)
    dc, Bdc = C.sb("dc", [128, 8], F32)
    rows, Brows = C.sb("rows", [64, 192], F32)
    rgw, Brgw = C.sb("rgw", [64, 128], F32)
    mats, Bmats = C.sb("mats", [64, 6, 64], F32)
    seg, Bseg = C.sb("seg", [128, TM], F32)
    idf, Bidf = C.sb("idf", [128, 128], F32)
    idb, Bidb = C.sb("idb", [128, 128], BF16)
    onesb, Bonesb = C.sb("onesb", [128, 128], BF16)
    S.dma("sync", lambda h: h.dma_start(out=cst, in_=cst_d), Bcst, writes=[Bcst])
    S.dma("sync", lambda h: h.dma_start(out=rows, in_=rows_d.partition_broadcast(64)), Brows, writes=[Brows])
    S.dma("sync", lambda h: h.dma_start(out=rgw, in_=rgw_d), Brgw, writes=[Brgw])
    S.dma("sync", lambda h: h.dma_start(out=mats, in_=mats_d), Bmats, writes=[Bmats])
    S.dma("sync", lambda h: h.dma_start(out=seg, in_=seg_d.partition_broadcast(128)), Bseg, writes=[Bseg])
    S.dma("sync", lambda h: h.dma_start(out=idf, in_=ident_d), Bidf, writes=[Bidf])
    V(lambda h: h.tensor_copy(out=idb, in_=idf), [Bidf], [Bidb])
    G(lambda h: h.memset(onesb, 1.0), [], [Bonesb])
    if use_lb:
        V(lambda h: h.tensor_tensor(out=dc[:, 4:5], in0=cst[:, 9:10], in1=cst[:, 8:9], op=ALU.subtract), [Bcst], [Bdc])
        ACT(dc[:, 0:1], dc[:, 4:5], AF.Sigmoid, [Bdc], [Bdc])
    else:
        V(lambda h: h.memset(dc[:, 0:1], 0.0), [], [Bdc])
    V(lambda h: h.tensor_scalar(out=dc[:, 1:2], in0=dc[:, 0:1], scalar1=-1.0, scalar2=1.0, op0=ALU.mult, op1=ALU.add), [Bdc], [Bdc])
    ACT(dc[:, 5:6], cst[:, 22:23], AF.Exp, [Bcst], [Bdc])
    V(lambda h: h.tensor_scalar(out=dc[:, 2:3], in0=dc[:, 5:6], scalar1=-1.0, scalar2=None, op0=ALU.mult), [Bdc], [Bdc])
    ACT(dc[:, 6:7], cst[:, 31:32], AF.Exp, [Bcst], [Bdc], scale=-1.0)
    ACT(dc[:, 7:8], dc[:, 6:7], AF.Ln, [Bdc], [Bdc], bias=1.0)
    V(lambda h: h.tensor_scalar(out=dc[:, 3:4], in0=dc[:, 7:8], scalar1=-8.0, scalar2=None, op0=ALU.mult), [Bdc], [Bdc])
    if MIX_STOP == -1:
        return
    base_off = C.off
    stg = [C.sb(f"mstg{i}", [128, MIX_NW], F32) for i in range(2)]
    for kc in range(NKC):
        sv, Bs = stg[kc % 2]
        S.dma("sync" if kc % 2 == 0 else "gpsimd", lambda h, sv=sv, kc=kc: h.dma_start(out=sv, in_=w_d[kc * 128:(kc + 1) * 128, :]), Bs, writes=[Bs])
        if kc % 2 == 0:
            V(lambda h, sv=sv, kc=kc: h.tensor_scalar(out=w[:, kc, :], in0=sv, scalar1=cst[:, kc:kc + 1], scalar2=None, op0=ALU.mult),
              [Bs, Bcst], [Bw])
        else:
            ACT(w[:, kc, :], sv, AF.Copy, [Bs, Bcst], [Bw], scale=cst[:, kc:kc + 1])
    S.barrier()
    C.reset(base_off)

    if MIX_STOP == -2:
        return
    Shg, BShg = C.sb("Shg", [64, 64], F32)
    Shgb, BShgb = C.sb("Shgb", [64, 64], BF16)
    Sgd, BSgd = C.sb("Sgd", [128, 128], F32)
    Sgdb, BSgdb = C.sb("Sgdb", [128, 128], BF16)
    hprev, Bhprev = C.sb("hprev", [64, 1], F32)
    xpq, Bxpq = C.sb("xpq", [128, 3 + TM], F32)
    xpk, Bxpk = C.sb("xpk", [128, 3 + TM], F32)
    xpv, Bxpv = C.sb("xpv", [128, 3 + TM], F32)
    xpr, Bxpr = C.sb("xpr", [64, 3 + TM], F32)
    Am, BAm = C.sb("Am", [64, TM], F32)
    Bm, BBm = C.sb("Bm", [64, TM], F32)
    for ap, B in [(Shg, BShg), (Shgb, BShgb), (Sgd, BSgd), (Sgdb, BSgdb), (hprev, Bhprev)]:
        G(lambda h, ap=ap: h.memset(ap, 0.0), [], [B])
    for ap, B in [(xpq, Bxpq), (xpk, Bxpk), (xpv, Bxpv), (xpr, Bxpr)]:
        G(lambda h, ap=ap: h.memset(ap[:, 0:3], 0.0), [], [B])
    G(lambda h: h.memset(Am[0:32, :], 1.0), [], [BAm])
    G(lambda h: h.memset(Bm[32:64, :], -1.0 / 32), [], [BBm])

    if MIX_STOP == -3:
        return
    xt = [C.sb(f"xt{i}", [128, NKC, TM], BF16) for i in range(2)]

    def f32t(name, P=128, n=TM):
        return C.sb(name, [P, n], F32)

    def b16t(name, P=128, n=TM):
        return C.sb(name, [P, n], BF16)
    qs, Bqs = f32t("qs", 64)
    fg, Bfg = f32t("fg", 64)
    logf, Blogf = f32t("logf", 64)
    kk, Bkk = f32t("kk", 64)
    cum, Bcum = f32t("cum", 64)
    ecum, Becum = f32t("ecum", 64)
    encum, Bencum = f32t("encum", 64)
    edec, Bedec = f32t("edec", 64)
    Qt, BQt = b16t("Qt", 64)
    Kt, BKt = b16t("Kt", 64)
    Kh, BKh = b16t("Kh", 64)
    Khtok, BKhtok = b16t("Khtok", 64)
    scTh, BscTh = b16t("scTh", 64)
    vhg, Bvhg = b16t("vhg", 64)
    gn, Bgn = C.sb("gn", [64, NCH, 192], F32)
    sqo, Bsqo = f32t("sqo", 64, NCH * 128)
    ssn, Bssn = C.sb("ssn", [64, 4 * NCH], F32)
    t1h, Bt1h = f32t("t1h", 64)
    yh, Byh = b16t("yh", 64)
    yTh, ByTh = b16t("yTh", 64)
    cq, Bcq = f32t("cq")
    ck, Bck = f32t("ck")
    cv, Bcv = f32t("cv")
    sqq, Bsqq = b16t("sqq")
    sqk, Bsqk = sqq, Bsqq
    rnq, Brnq = f32t("rnq")
    rnk, Brnk = rnq, Brnq
    qn, Bqn = f32t("qn")
    kn, Bkn = f32t("kn")
    betaB, BbetaB = f32t("betaB")
    gB, BgB = f32t("gB")
    cumB, BcumB = f32t("cumB")
    ecumB, BecumB = f32t("ecumB")
    edecB, BedecB = f32t("edecB")
    qT, BqT = b16t("qT")
    qeT, BqeT = b16t("qeT")
    kT, BkT = b16t("kT")
    kbT, BkbT = b16t("kbT")
    kbeT, BkbeT = b16t("kbeT")
    kdecT, BkdecT = b16t("kdecT")
    vT, BvT = b16t("vT")
    bvT, BbvT = b16t("bvT")
    kbe, Bkbe = C.sb("kbe", [64, NCH, 128], BF16)
    kdec, Bkdec = C.sb("kdec", [64, NCH, 128], BF16)
    bv, Bbv = C.sb("bv", [64, NCH, 128], BF16)
    dtmp, Bdtmp = f32t("dtmp", 64)
    DTm, BDTm = f32t("DTm", 64)
    DTs, BDTs = f32t("DTs", 64)
    Dm, BDm = f32t("Dm", 64)
    Ds, BDs = Dm, BDm
    scT, BscT = b16t("scT", 64)
    Xs = [C.sb(f"Xs{i}", [64, TM], SD) for i in range(2)]
    Ys = [C.sb(f"Ys{i}", [64, TM], SD) for i in range(2)]
    Pm, BPm = C.sb("Pm", [64, TM], SD)
    Pb, BPb = b16t("Pb", 64)
    nwT, BnwT = b16t("nwT")
    vnew = [C.sb(f"vnew{i}", [64, 128], BF16) for i in range(2)]
    t1g, Bt1g = f32t("t1g", 64, NCH * 128)
    yg, Byg = b16t("yg", 64, NCH * 128)
    yTg, ByTg = b16t("yTg")
    xc, Bxc = f32t("xc", 64)
    rr, Brr = f32t("rr", 64)
    ii, Bii = f32t("ii", 64)
    aa, Baa = f32t("aa", 64)
    mm, Bmm = f32t("mm", 64)
    bx, Bbx = f32t("bx", 64)
    hh = [C.sb(f"hh{i}", [64, TM], F32) for i in range(2)]
    gate, Bgate = f32t("gate", 64)
    yTr, ByTr = b16t("yTr", 64)

    save_rot = C.prot
    C.prot = [0, 1, 2, 3, 4]
    C.pnext = 0
    Ohg, BOhg = C.psum[5], C.pbuf[5]
    Og = [(C.psum[6], C.pbuf[6]), (C.psum[7], C.pbuf[7])]

    def fproj(xv, Bx, col, M):
        pb, Bpb = C.pbank()
        for kc in range(NKC):
            T(lambda h, kc=kc, pb=pb: h.matmul(pb[0:M, :], lhsT=w[:, kc, col:col + M], rhs=xv[:, kc, :],
                                               start=(kc == 0), stop=(kc == NKC - 1)), [Bw, Bx], [Bpb])
        return pb, Bpb

    ntiles = L // TM
    cross = {}
    for nm_, (ap_, B_) in dict(scTh=(scTh, BscTh), vhg=(vhg, Bvhg), Qt=(Qt, BQt), Khtok=(Khtok, BKhtok), ecum=(ecum, Becum),
                                bv=(bv, Bbv), qeT=(qeT, BqeT), scT=(scT, BscT), kdec=(kdec, Bkdec), ecumB=(ecumB, BecumB),
                                gn=(gn, Bgn), kbe=(kbe, Bkbe)).items():
        ap2_, B2_ = C.sb(nm_ + "_b", list(ap_.shape), ap_.dtype)
        cross[nm_] = [(ap_, B_), (ap2_, B2_)]
    Xs0 = [Xs[0], C.sb("Xs0b", [64, TM], SD)]
    Ys0 = [Ys[0], C.sb("Ys0b", [64, TM], SD)]
    def tile_A(it):
        scTh, BscTh = cross["scTh"][it % 2]
        vhg, Bvhg = cross["vhg"][it % 2]
        Qt, BQt = cross["Qt"][it % 2]
        Khtok, BKhtok = cross["Khtok"][it % 2]
        ecum, Becum = cross["ecum"][it % 2]
        bv, Bbv = cross["bv"][it % 2]
        qeT, BqeT = cross["qeT"][it % 2]
        scT, BscT = cross["scT"][it % 2]
        kdec, Bkdec = cross["kdec"][it % 2]
        ecumB, BecumB = cross["ecumB"][it % 2]
        gn, Bgn = cross["gn"][it % 2]
        kbe, Bkbe = cross["kbe"][it % 2]
        Xl = [Xs0[it % 2], Xs[1]]
        Yl = [Ys0[it % 2], Ys[1]]
        t0 = it * TM
        yield
        xv, Bx = xt[it % 2]
        S.dma("sync", lambda h, xv=xv, t0=t0: h.dma_start(
            out=xv, in_=xT_src(t0, TM).rearrange("(kc p) t -> p kc t", p=128)), Bx, writes=[Bx],
            reads=(xT_src.dbuf(t0) if hasattr(xT_src, "dbuf") else []))

        yield
        pb, Bpb = fproj(xv, Bx, C_HQ, 64)
        ACT(qs, pb[0:64, :], AF.Silu, [Bpb], [Bqs])
        yield
        pb, Bpb = fproj(xv, Bx, C_HF, 64)
        ACT(fg, pb[0:64, :], AF.Sigmoid, [Bpb], [Bfg])
        yield
        pb, Bpb = fproj(xv, Bx, C_GB, 128)
        ACT(betaB, pb, AF.Sigmoid, [Bpb], [BbetaB])
        yield
        pb, Bpb = fproj(xv, Bx, C_GA, 128)
        ACT(gB, pb, AF.Exp, [Bpb, Bcst], [BgB], bias=cst[:, 23:24])
        yield
        ACT(gB, gB, AF.Ln, [BgB], [BgB], bias=1.0)
        for col, xp, Bxp in [(C_GQ, xpq, Bxpq), (C_GK, xpk, Bxpk), (C_GV, xpv, Bxpv)]:
            pb, Bpb = fproj(xv, Bx, col, 128)
            ACT(xp[:, 3:3 + TM], pb, AF.Copy, [Bpb], [Bxp])
        yield
        pb, Bpb = fproj(xv, Bx, C_RX, 64)
        ACT(xpr[:, 3:3 + TM], pb[0:64, :], AF.Copy, [Bpb], [Bxpr])
        yield
        pb, Bpb = fproj(xv, Bx, C_RG, 64)
        ACT(gate, pb[0:64, :], AF.Gelu_apprx_tanh, [Bpb], [Bgate])
        yield
        for cp in range(NCH // 2):
            pb, Bpb = C.pbank()
            p3 = pb[0:64, :].rearrange("p (c j) -> p c j", j=256)
            for cc in range(2):
                c = cp * 2 + cc
                for kc in range(NKC):
                    T(lambda h, kc=kc, c=c, cc=cc, p3=p3, xv=xv: h.matmul(
                        p3[:, cc, :], lhsT=xv[:, kc, c * 64:(c + 1) * 64], rhs=w[:, kc, C_T:C_T + 256],
                        start=(kc == 0), stop=(kc == NKC - 1)), [Bw, Bx], [Bpb])
            V(lambda h, cp=cp, p3=p3: h.tensor_copy(out=c3(vhg)[:, cp * 2:cp * 2 + 2, :], in_=p3[:, :, 0:64]), [Bpb], [Bvhg])
            for cc in range(2):
                ACT(gn[:, cp * 2 + cc, :], p3[:, cc, 64:256], AF.Silu, [Bpb], [Bgn])
        V(lambda h: h.tensor_tensor(out=gn, in0=gn, in1=rows.unsqueeze(1).to_broadcast([64, NCH, 192]), op=ALU.mult),
          [Bgn, Brows], [Bgn])

        def hg_chain():
            yield
            if use_lb:
                V(lambda h: h.tensor_scalar(out=fg, in0=fg, scalar1=dc[0:64, 1:2], scalar2=dc[0:64, 0:1], op0=ALU.mult, op1=ALU.add),
                  [Bfg, Bdc], [Bfg])
            ACT(logf, fg, AF.Ln, [Bfg], [Blogf])
            yield
            V(lambda h: h.tensor_scalar(out=kk, in0=fg, scalar1=-1.0, scalar2=1.0, op0=ALU.mult, op1=ALU.add), [Bfg], [Bkk])
            V(lambda h: h.tensor_tensor_scan(out=cum, data0=seg[0:64, :], data1=logf, initial=0.0, op0=ALU.mult, op1=ALU.add),
              [Bseg, Blogf], [Bcum])
            yield
            ACT(ecum, cum, AF.Exp, [Bcum], [Becum])
            ACT(encum, cum, AF.Exp, [Bcum], [Bencum], scale=-1.0)
            yield
            V(lambda h: h.tensor_tensor(out=c3(edec), in0=c3(cum)[:, :, 63:64].to_broadcast([64, NCH, 64]), in1=c3(cum), op=ALU.subtract),
              [Bcum], [Bedec])
            ACT(edec, edec, AF.Exp, [Bedec], [Bedec])
            yield
            V(lambda h: h.scalar_tensor_tensor(out=Qt, in0=qs, scalar=0.125, in1=ecum, op0=ALU.mult, op1=ALU.mult), [Bqs, Becum], [BQt])
            V(lambda h: h.tensor_tensor(out=Kt, in0=kk, in1=encum, op=ALU.mult), [Bkk, Bencum], [BKt])
            yield
            V(lambda h: h.tensor_tensor(out=Kh, in0=kk, in1=edec, op=ALU.mult), [Bkk, Bedec], [BKh])
            pb, Bpb = C.pbank()
            yield
            for c in range(NCH):
                T(lambda h, c=c, pb=pb: h.matmul(pb[0:64, c * 64:(c + 1) * 64], lhsT=Kt[:, c * 64:(c + 1) * 64], rhs=Qt[:, c * 64:(c + 1) * 64],
                                                 start=True, stop=True), [BKt, BQt], [Bpb])
            V(lambda h, pb=pb: h.tensor_tensor(out=c3(scTh), in0=c3(pb[0:64, :]), in1=mats[:, 0:1, :].to_broadcast([64, NCH, 64]), op=ALU.mult),
              [Bpb, Bmats], [BscTh])
            yield
            pb, Bpb = C.pbank()
            pbb = pb.bitcast(BF16)
            yield
            for c in range(NCH):
                T(lambda h, c=c, pbb=pbb: h.transpose(out=pbb[0:64, c * 64:(c + 1) * 64], in_=Kh[:, c * 64:(c + 1) * 64], identity=idb[0:64, 0:64]),
                  [BKh, Bidb], [Bpb])
            ACT(Khtok, pbb[0:64, 0:TM], AF.Copy, [Bpb], [BKhtok])

            yield
        def gd_chain():
            yield
            for xp, Bxp, co, Bco, cb in [(xpq, Bxpq, cq, Bcq, 10), (xpk, Bxpk, ck, Bck, 14), (xpv, Bxpv, cv, Bcv, 18)]:
                eng = V
                eng(lambda h, xp=xp, co=co, cb=cb: h.tensor_scalar(out=co, in0=xp[:, 0:TM], scalar1=cst[:, cb:cb + 1], scalar2=None, op0=ALU.mult),
                    [Bxp, Bcst], [Bco])
                for j in range(1, 4):
                    eng(lambda h, xp=xp, co=co, cb=cb, j=j: h.scalar_tensor_tensor(
                        out=co, in0=xp[:, j:j + TM], scalar=cst[:, cb + j:cb + j + 1], in1=co, op0=ALU.mult, op1=ALU.add),
                        [Bxp, Bcst, Bco], [Bco])
                G(lambda h, xp=xp: h.tensor_copy(out=xp[:, 0:3], in_=xp[:, TM:TM + 3]), [Bxp], [Bxp])
            ACT(cq, cq, AF.Silu, [Bcq], [Bcq])
            yield
            ACT(ck, ck, AF.Silu, [Bck], [Bck])
            ACT(vT, cv, AF.Silu, [Bcv], [BvT])
            yield
            for src, Bsrc, sq_, Bsq_, rn, Brn, dst, Bdst, scl in [(cq, Bcq, sqq, Bsqq, rnq, Brnq, qn, Bqn, 128 ** -0.5),
                                                                 (ck, Bck, sqk, Bsqk, rnk, Brnk, kn, Bkn, 1.0)]:
                ACT(sq_, src, AF.Square, [Bsrc], [Bsq_])
                pb, Bpb = C.pbank()
                T(lambda h, pb=pb, sq_=sq_: h.matmul(pb, lhsT=onesb, rhs=sq_, start=True, stop=True), [Bonesb, Bsq_], [Bpb])
                ACT(rn, pb, AF.Sqrt, [Bpb], [Brn], bias=EPS)
                V(lambda h, rn=rn: h.reciprocal(out=rn, in_=rn), [Brn], [Brn])
                V(lambda h, src=src, rn=rn, dst=dst, scl=scl: h.scalar_tensor_tensor(out=dst, in0=src, scalar=scl, in1=rn, op0=ALU.mult, op1=ALU.mult),
                  [Bsrc, Brn], [Bdst])
            V(lambda h: h.tensor_scalar(out=gB, in0=gB, scalar1=dc[:, 2:3], scalar2=None, op0=ALU.mult), [BgB, Bdc], [BgB])
            yield
            V(lambda h: h.tensor_tensor_scan(out=cumB, data0=seg, data1=gB, initial=0.0, op0=ALU.mult, op1=ALU.add), [Bseg, BgB], [BcumB])
            ACT(ecumB, cumB, AF.Exp, [BcumB], [BecumB])
            yield
            V(lambda h: h.tensor_tensor(out=c3(edecB), in0=c3(cumB)[:, :, 63:64].to_broadcast([128, NCH, 64]), in1=c3(cumB), op=ALU.subtract),
              [BcumB], [BedecB])
            ACT(edecB, edecB, AF.Exp, [BedecB], [BedecB])
            yield
            G(lambda h: h.tensor_copy(out=Am[32:64, :], in_=cumB[32:64, :]), [BcumB], [BAm])
            G(lambda h: h.tensor_scalar(out=Bm[0:32, :], in0=cumB[0:32, :], scalar1=1.0 / 32, scalar2=None, op0=ALU.mult), [BcumB], [BBm])
            yield
            ACT(qT, qn, AF.Copy, [Bqn], [BqT])
            V(lambda h: h.tensor_tensor(out=qeT, in0=qn, in1=ecumB, op=ALU.mult), [Bqn, BecumB], [BqeT])
            yield
            ACT(kT, kn, AF.Copy, [Bkn], [BkT])
            V(lambda h: h.tensor_tensor(out=kn, in0=kn, in1=betaB, op=ALU.mult), [Bkn, BbetaB], [Bkn])
            yield
            ACT(kbT, kn, AF.Copy, [Bkn], [BkbT])
            V(lambda h: h.tensor_tensor(out=kbeT, in0=kn, in1=ecumB, op=ALU.mult), [Bkn, BecumB], [BkbeT])
            yield
            V(lambda h: h.tensor_tensor(out=kdecT, in0=kT, in1=edecB, op=ALU.mult), [BkT, BedecB], [BkdecT])
            V(lambda h: h.tensor_tensor(out=bvT, in0=vT, in1=betaB, op=ALU.mult), [BvT, BbetaB], [BbvT])
            yield
            for srcT, BsrcT, dst, Bdst in [(kbeT, BkbeT, kbe, Bkbe), (kdecT, BkdecT, kdec, Bkdec), (bvT, BbvT, bv, Bbv)]:
                pb, Bpb = C.pbank()
                pbb = pb.bitcast(BF16)
                for c in range(NCH):
                    T(lambda h, c=c, pbb=pbb, srcT=srcT: h.transpose(out=pbb[0:64, c * 128:(c + 1) * 128], in_=srcT[:, c * 64:(c + 1) * 64], identity=idb),
                      [BsrcT, Bidb], [Bpb])
                ACT(dst.rearrange("p c j -> p (c j)"), pbb[0:64, :], AF.Copy, [Bpb], [Bdst])
            for lhs_, Blhs, rhs_, Brhs, mi, si, Dfull, BDfull, Dstr, BDstr in [
                    (Am, BAm, Bm, BBm, 1, 3, DTm, BDTm, DTs, BDTs), (Bm, BBm, Am, BAm, 2, 4, Dm, BDm, Ds, BDs)]:
                pb, Bpb = C.pbank()
                for c in range(NCH):
                    T(lambda h, c=c, pb=pb, lhs_=lhs_, rhs_=rhs_: h.matmul(
                        pb[0:64, c * 64:(c + 1) * 64], lhsT=lhs_[:, c * 64:(c + 1) * 64], rhs=rhs_[:, c * 64:(c + 1) * 64],
                        start=True, stop=True), [Blhs, Brhs], [Bpb])
                V(lambda h, pb=pb, mi=mi: h.tensor_tensor(out=c3(dtmp), in0=c3(pb[0:64, :]), in1=mats[:, mi:mi + 1, :].to_broadcast([64, NCH, 64]), op=ALU.add),
                  [Bpb, Bmats], [Bdtmp])
                ACT(Dfull, dtmp, AF.Exp, [Bdtmp], [BDfull])
                V(lambda h, Dfull=Dfull, Dstr=Dstr, si=si: h.tensor_tensor(
                    out=c3(Dstr), in0=c3(Dfull), in1=mats[:, si:si + 1, :].to_broadcast([64, NCH, 64]), op=ALU.mult), [BDfull, Bmats], [BDstr])
            yield
            pb, Bpb = C.pbank()
            for c in range(NCH):
                T(lambda h, c=c, pb=pb: h.matmul(pb[0:64, c * 64:(c + 1) * 64], lhsT=kT[:, c * 64:(c + 1) * 64], rhs=qT[:, c * 64:(c + 1) * 64],
                                                 start=True, stop=True), [BkT, BqT], [Bpb])
            yield
            V(lambda h, pb=pb: h.tensor_tensor(out=scT, in0=pb[0:64, :], in1=DTm, op=ALU.mult), [Bpb, BDTm], [BscT])
            Y0, BY0 = Yl[0]
            yield
            X0, BX0 = Xl[0]
            pb, Bpb = C.pbank()
            yield
            for c in range(NCH):
                T(lambda h, c=c, pb=pb: h.matmul(pb[0:64, c * 64:(c + 1) * 64], lhsT=kT[:, c * 64:(c + 1) * 64], rhs=kbT[:, c * 64:(c + 1) * 64],
                                                 start=True, stop=True), [BkT, BkbT], [Bpb])
            V(lambda h, pb=pb: h.scalar_tensor_tensor(out=Y0, in0=pb[0:64, :], scalar=-1.0, in1=DTs, op0=ALU.mult, op1=ALU.mult), [Bpb, BDTs], [BY0])
            yield
            pb, Bpb = C.pbank()
            for c in range(NCH):
                T(lambda h, c=c, pb=pb: h.matmul(pb[0:64, c * 64:(c + 1) * 64], lhsT=kbT[:, c * 64:(c + 1) * 64], rhs=kT[:, c * 64:(c + 1) * 64],
                                                 start=True, stop=True), [BkT, BkbT], [Bpb])
            yield
            V(lambda h, pb=pb: h.scalar_tensor_tensor(out=X0, in0=pb[0:64, :], scalar=-1.0, in1=Ds, op0=ALU.mult, op1=ALU.mult), [Bpb, BDs], [BX0])
            yield
        def rg_chain():
            V(lambda h: h.tensor_scalar(out=xc, in0=xpr[:, 0:TM], scalar1=cst[0:64, 24:25], scalar2=cst[0:64, 28:29], op0=ALU.mult, op1=ALU.add),
              [Bxpr, Bcst], [Bxc])
            yield
            for j in range(1, 4):
                V(lambda h, j=j: h.scalar_tensor_tensor(out=xc, in0=xpr[:, j:j + TM], scalar=cst[0:64, 24 + j:25 + j], in1=xc, op0=ALU.mult, op1=ALU.add),
                  [Bxpr, Bcst, Bxc], [Bxc])
            G(lambda h: h.tensor_copy(out=xpr[:, 0:3], in_=xpr[:, TM:TM + 3]), [Bxpr], [Bxpr])
            yield
            pb, Bpb = C.pbank()
            T(lambda h, pb=pb: h.matmul(pb[0:64, :], lhsT=rgw[:, 0:64], rhs=xc, start=True, stop=True), [Brgw, Bxc], [Bpb])
            yield
            ACT(rr, pb[0:64, :], AF.Sigmoid, [Bpb, Bcst], [Brr], bias=cst[0:64, 29:30])
            pb, Bpb = C.pbank()
            yield
            T(lambda h, pb=pb: h.matmul(pb[0:64, :], lhsT=rgw[:, 64:128], rhs=xc, start=True, stop=True), [Brgw, Bxc], [Bpb])
            ACT(ii, pb[0:64, :], AF.Sigmoid, [Bpb, Bcst], [Bii], bias=cst[0:64, 30:31])
            yield
            ACT(aa, rr, AF.Exp, [Brr, Bdc], [Baa], scale=dc[0:64, 3:4])
            V(lambda h: h.tensor_tensor(out=mm, in0=aa, in1=aa, op=ALU.mult), [Baa], [Bmm])
            yield
            ACT(mm, mm, AF.Sqrt, [Bmm], [Bmm], scale=-1.0, bias=1.0)
            V(lambda h: h.tensor_tensor(out=bx, in0=ii, in1=xc, op=ALU.mult), [Bii, Bxc], [Bbx])
            yield
            V(lambda h: h.tensor_tensor(out=bx, in0=bx, in1=mm, op=ALU.mult), [Bbx, Bmm], [Bbx])
            hv, Bh = hh[it % 2]
            yield
            V(lambda h, hv=hv: h.tensor_tensor_scan(out=hv, data0=aa, data1=bx, initial=hprev[:, 0:1], op0=ALU.mult, op1=ALU.add),
              [Baa, Bbx, Bhprev], [Bh])
            V(lambda h, hv=hv: h.tensor_copy(out=hprev, in_=hv[:, TM - 1:TM]), [Bh], [Bhprev])
            yield
            V(lambda h, hv=hv: h.tensor_tensor(out=yTr, in0=hv, in1=gate, op=ALU.mult), [Bh, Bgate], [ByTr])
            S.dma("sync", lambda h, t0=t0: h.dma_start(out=yT_out[192:256, t0:t0 + TM], in_=yTr), ByTr, reads=[ByTr],
                  writes=(yT_out.dbuf(t0) if hasattr(yT_out, "dbuf") else []))

            yield
        chains = [gd_chain(), hg_chain()]
        while chains:
            for g_ in list(chains):
                try:
                    next(g_)
                except StopIteration:
                    chains.remove(g_)
                yield
        yield from rg_chain()
        yield

    def tile_B(it):
        t0 = it * TM
        scTh, BscTh = cross["scTh"][it % 2]
        vhg, Bvhg = cross["vhg"][it % 2]
        Qt, BQt = cross["Qt"][it % 2]
        Khtok, BKhtok = cross["Khtok"][it % 2]
        ecum, Becum = cross["ecum"][it % 2]
        bv, Bbv = cross["bv"][it % 2]
        qeT, BqeT = cross["qeT"][it % 2]
        scT, BscT = cross["scT"][it % 2]
        kdec, Bkdec = cross["kdec"][it % 2]
        ecumB, BecumB = cross["ecumB"][it % 2]
        gn, Bgn = cross["gn"][it % 2]
        kbe, Bkbe = cross["kbe"][it % 2]
        Xl = [Xs0[it % 2], Xs[1]]
        Yl = [Ys0[it % 2], Ys[1]]
        def gdB():
            V(lambda h: h.tensor_tensor(out=c3(Pm), in0=c3(Yl[0][0]), in1=mats[:, 5:6, :].to_broadcast([64, NCH, 64]), op=ALU.add), [Yl[0][1], Bmats], [BPm])
            yield
            cur = 0
            for r in range(1, 6):
                yield
                Xc, BXc = Xl[cur]
                Yc, BYc = Yl[cur]
                Xn, BXn = Xl[1 - cur]
                Yn, BYn = Yl[1 - cur]
                pbx, Bpbx = C.pbank()
                for c in range(NCH):
                    sl = slice(c * 64, (c + 1) * 64)
                    T(lambda h, sl=sl, pbx=pbx, Xc=Xc, Yc=Yc: h.matmul(pbx[0:64, sl], lhsT=Yc[:, sl], rhs=Xc[:, sl], start=True, stop=True),
                      [BXc, BYc], [Bpbx])
                ACT(Xn, pbx[0:64, :], AF.Copy, [Bpbx], [BXn])
                if r < 5:
                    pby, Bpby = C.pbank()
                    for c in range(NCH):
                        sl = slice(c * 64, (c + 1) * 64)
                        T(lambda h, sl=sl, pby=pby, Xc=Xc, Yc=Yc: h.matmul(pby[0:64, sl], lhsT=Xc[:, sl], rhs=Yc[:, sl], start=True, stop=True),
                          [BXc, BYc], [Bpby])
                    V(lambda h, pby=pby, Yn=Yn: h.tensor_copy(out=Yn, in_=pby[0:64, :]), [Bpby], [BYn])
                pbp, Bpbp = C.pbank()
                for c in range(NCH):
                    sl = slice(c * 64, (c + 1) * 64)
                    T(lambda h, sl=sl, pbp=pbp, Xn=Xn: h.matmul(pbp[0:64, sl], lhsT=Xn[:, sl], rhs=Pm[:, sl], start=True, stop=True),
                      [BXn, BPm], [Bpbp])
                V(lambda h, pbp=pbp: h.tensor_tensor(out=Pm, in0=Pm, in1=pbp[0:64, :], op=ALU.add), [Bpbp, BPm], [BPm])
                cur = 1 - cur
            yield
            ACT(Pb, Pm, AF.Copy, [BPm], [BPb])
            pb, Bpb = C.pbank()
            yield
            for c in range(NCH):
                T(lambda h, c=c, pb=pb: h.matmul(pb[:, c * 64:(c + 1) * 64], lhsT=kbe[:, c, :], rhs=Pb[:, c * 64:(c + 1) * 64], start=True, stop=True),
                  [Bkbe, BPb], [Bpb])
            ACT(nwT, pb, AF.Copy, [Bpb], [BnwT], scale=-1.0)

            for c in range(NCH):
                sl = slice(c * 64, (c + 1) * 64)
                yield
                pa, Bpa = C.pbank()
                T(lambda h, sl=sl, pa=pa, c=c: h.matmul(pa[0:64, 0:128], lhsT=Pb[:, sl], rhs=bv[:, c, :], start=True, stop=False), [BPb, Bbv], [Bpa])
                T(lambda h, sl=sl, pa=pa: h.matmul(pa[0:64, 0:128], lhsT=nwT[:, sl], rhs=Sgdb, start=False, stop=True), [BnwT, BSgdb], [Bpa])
                vn, Bvn = vnew[c % 2]
                ACT(vn, pa[0:64, 0:128], AF.Copy, [Bpa], [Bvn])
                og, BOg = Og[c // 4]
                osl = slice((c % 4) * 128, (c % 4 + 1) * 128)
                T(lambda h, sl=sl, og=og, osl=osl: h.matmul(og[0:64, osl], lhsT=qeT[:, sl], rhs=Sgdb, start=True, stop=False), [BqeT, BSgdb], [BOg])
                T(lambda h, sl=sl, og=og, osl=osl, vn=vn: h.matmul(og[0:64, osl], lhsT=scT[:, sl], rhs=vn, start=False, stop=True), [BscT, Bvn], [BOg])
                pn, Bpn = C.pbank()
                T(lambda h, pn=pn, vn=vn, c=c: h.matmul(pn[:, 0:128], lhsT=kdec[:, c, :], rhs=vn, start=True, stop=True), [Bkdec, Bvn], [Bpn])
                elg = c3(ecumB)[:, c, 63:64]
                V(lambda h, pn=pn, elg=elg: h.scalar_tensor_tensor(out=Sgdb, in0=Sgd, scalar=elg, in1=pn[:, 0:128], op0=ALU.mult, op1=ALU.add),
                  [BSgd, BecumB, Bpn], [BSgdb])
                V(lambda h, pn=pn, elg=elg: h.scalar_tensor_tensor(out=Sgd, in0=Sgd, scalar=elg, in1=pn[:, 0:128], op0=ALU.mult, op1=ALU.add),
                  [BSgd, BecumB, Bpn], [BSgd])
            for hb in range(2):
                og, BOg = Og[hb]
                seg_ = slice(hb * 512, (hb + 1) * 512)
                ACT(sqo[:, seg_], og[0:64, :], AF.Square, [BOg], [Bsqo])
            yield
            V(lambda h: h.tensor_reduce(out=ssn[:, 2 * NCH:3 * NCH], in_=c3(sqo, 128), axis=AX.X, op=ALU.add), [Bsqo], [Bssn])
            ACT(ssn[:, 3 * NCH:4 * NCH], ssn[:, 2 * NCH:3 * NCH], AF.Sqrt, [Bssn], [Bssn], scale=1.0 / 128, bias=EPS)
            yield
            V(lambda h: h.reciprocal(out=ssn[:, 2 * NCH:3 * NCH], in_=ssn[:, 3 * NCH:4 * NCH]), [Bssn], [Bssn])
            for hb in range(2):
                og, BOg = Og[hb]
                seg_ = slice(hb * 512, (hb + 1) * 512)
                V(lambda h, og=og, seg_=seg_, hb=hb: h.tensor_tensor(
                    out=c3(t1g[:, seg_], 128), in0=c3(og[0:64, :], 128),
                    in1=ssn[:, 2 * NCH + hb * 4:2 * NCH + hb * 4 + 4].unsqueeze(2).to_broadcast([64, 4, 128]), op=ALU.mult),
                    [BOg, Bssn], [Bt1g])
            yield
            V(lambda h: h.tensor_tensor(out=c3(yg, 128), in0=c3(t1g, 128), in1=gn[:, :, 64:192], op=ALU.mult), [Bt1g, Bgn], [Byg])
            pb, Bpb = C.pbank()
            yield
            pbb = pb.bitcast(BF16)
            for c in range(NCH):
                T(lambda h, c=c, pbb=pbb: h.transpose(out=pbb[:, c * 64:(c + 1) * 64], in_=yg[:, c * 128:(c + 1) * 128], identity=idb[0:64, 0:64]),
                  [Byg, Bidb], [Bpb])
            yield
            ACT(yTg, pbb[:, 0:TM], AF.Copy, [Bpb], [ByTg])
            S.dma("sync", lambda h, t0=t0: h.dma_start(out=yT_out[64:192, t0:t0 + TM], in_=yTg), ByTg, reads=[ByTg],
                  writes=(yT_out.dbuf(t0) if hasattr(yT_out, "dbuf") else []))
            yield
            yield
        def hgB():
            for c in range(NCH):
                sl = slice(c * 64, (c + 1) * 64)
                yield
                T(lambda h, sl=sl: h.matmul(Ohg[0:64, sl], lhsT=scTh[:, sl], rhs=vhg[:, sl], start=True, stop=False), [BscTh, Bvhg], [BOhg])
                T(lambda h, sl=sl: h.matmul(Ohg[0:64, sl], lhsT=Qt[:, sl], rhs=Shgb, start=False, stop=True), [BQt, BShgb], [BOhg])
                pb, Bpb = C.pbank()
                T(lambda h, sl=sl, pb=pb: h.matmul(pb[0:64, 0:64], lhsT=Khtok[:, sl], rhs=vhg[:, sl], start=True, stop=True), [BKhtok, Bvhg], [Bpb])
                el = c3(ecum)[:, c, 63:64]
                V(lambda h, pb=pb, el=el: h.scalar_tensor_tensor(out=Shgb, in0=Shg, scalar=el, in1=pb[0:64, 0:64], op0=ALU.mult, op1=ALU.add),
                  [BShg, Becum, Bpb], [BShgb])
                V(lambda h, pb=pb, el=el: h.scalar_tensor_tensor(out=Shg, in0=Shg, scalar=el, in1=pb[0:64, 0:64], op0=ALU.mult, op1=ALU.add),
                  [BShg, Becum, Bpb], [BShg])
                yield
            yield
            ACT(sqo[:, 0:TM], Ohg[0:64, :], AF.Square, [BOhg], [Bsqo])
            V(lambda h: h.tensor_reduce(out=ssn[:, 0:NCH], in_=c3(sqo[:, 0:TM]), axis=AX.X, op=ALU.add), [Bsqo], [Bssn])
            yield
            ACT(ssn[:, NCH:2 * NCH], ssn[:, 0:NCH], AF.Sqrt, [Bssn], [Bssn], scale=1.0 / 64, bias=EPS)
            V(lambda h: h.reciprocal(out=ssn[:, 0:NCH], in_=ssn[:, NCH:2 * NCH]), [Bssn], [Bssn])
            yield
            V(lambda h: h.tensor_tensor(out=c3(t1h), in0=c3(Ohg[0:64, :]), in1=ssn[:, 0:NCH].unsqueeze(2).to_broadcast([64, NCH, 64]), op=ALU.mult),
              [BOhg, Bssn], [Bt1h])
            V(lambda h: h.tensor_tensor(out=c3(yh), in0=c3(t1h), in1=gn[:, :, 0:64], op=ALU.mult), [Bt1h, Bgn], [Byh])
            yield
            pb, Bpb = C.pbank()
            pbb = pb.bitcast(BF16)
            yield
            for c in range(NCH):
                sl = slice(c * 64, (c + 1) * 64)
                T(lambda h, sl=sl, pbb=pbb: h.transpose(out=pbb[0:64, sl], in_=yh[:, sl], identity=idb[0:64, 0:64]), [Byh, Bidb], [Bpb])
            ACT(yTh, pbb[0:64, 0:TM], AF.Copy, [Bpb], [ByTh])
            yield
            S.dma("sync", lambda h, t0=t0: h.dma_start(out=yT_out[0:64, t0:t0 + TM], in_=yTh), ByTh, reads=[ByTh],
                  writes=(yT_out.dbuf(t0) if hasattr(yT_out, "dbuf") else []))
            yield
        chains = [gdB(), hgB()]
        while chains:
            for g_ in list(chains):
                try:
                    next(g_)
                except StopIteration:
                    chains.remove(g_)
                yield

    def interleave(g1, g2):
        gens = [g for g in (g1, g2) if g is not None]
        while gens:
            for g in list(gens):
                try:
                    next(g)
                except StopIteration:
                    gens.remove(g)
    interleave(tile_A(0), None)
    for it in range(ntiles):
        interleave(tile_B(it), tile_A(it + 1) if it + 1 < ntiles else None)
        if on_tile is not None:
            on_tile(it)
    C.prot = save_rot
    C.pnext = 0
    S.barrier()


def phase_wout(C, NT, x_in, x_out, yT_src, wo_d, ysrcs=None, msk_d=None):
    S = C.S
    C.reset()
    TT = 512
    wo, Bwo = C.sb("wo", [128, NKC, D], BF16)
    base_off = C.off
    stg = [C.sb(f"ostg{i}", [128, D], F32) for i in range(2)]
    for kc in range(NKC):
        sv, Bs = stg[kc % 2]
        S.dma("sync" if kc % 2 == 0 else "gpsimd", lambda h, sv=sv, kc=kc: h.dma_start(out=sv, in_=wo_d[kc * 128:(kc + 1) * 128, :]), Bs, writes=[Bs])
        if kc % 2 == 0:
            S.op("vector", lambda h, sv=sv, kc=kc: h.tensor_copy(out=wo[:, kc, :], in_=sv), reads=[Bs], writes=[Bwo])
        else:
            S.op("scalar", lambda h, sv=sv, kc=kc: h.activation(out=wo[:, kc, :], in_=sv, func=AF.Copy), reads=[Bs], writes=[Bwo])
    S.barrier()
    C.reset(base_off)
    yt = [C.sb(f"yt{i}", [128, NKC, TT], BF16) for i in range(2)]
    if ysrcs is not None:
        ycand = [C.sb(f"ycand{i}", [128, NKC, TT], BF16) for i in range(4)]
        msk, Bmsk = C.sb("msk", [128, 4], F32)
        S.dma("sync", lambda h: h.dma_start(out=msk, in_=msk_d), Bmsk, writes=[Bmsk])
    xs = [C.sb(f"oxs{i}", [128, TT // 128, D], F32) for i in range(2)]
    os_ = [C.sb(f"oos{i}", [128, D], F32) for i in range(2)]
    for it in range(NT // TT):
        t0 = it * TT
        yv, By = yt[it % 2]
        xv, Bx = xs[it % 2]
        if ysrcs is None:
            S.dma("sync", lambda h, yv=yv, t0=t0: h.dma_start(
                out=yv, in_=yT_src(t0, TT).rearrange("(kc p) t -> p kc t", p=128)), By, writes=[By])
        else:
            for jc in range(4):
                cv_, Bc_ = ycand[jc]
                S.dma("sync", lambda h, cv_=cv_, jc=jc, t0=t0: h.dma_start(
                    out=cv_, in_=ysrcs[jc](t0, TT).rearrange("(kc p) t -> p kc t", p=128)), Bc_, writes=[Bc_],
                    reads=(ysrcs[jc].dbuf(t0) if hasattr(ysrcs[jc], "dbuf") else []))
            yf = yv.rearrange("p a b -> p (a b)")
            for jc in range(4):
                cv_, Bc_ = ycand[jc]
                cf = cv_.rearrange("p a b -> p (a b)")
                if jc == 0:
                    S.op("vector", lambda h, cf=cf, yf=yf: h.tensor_scalar(out=yf, in0=cf, scalar1=msk[:, 0:1], scalar2=None, op0=ALU.mult),
                         reads=[Bc_, Bmsk], writes=[By])
                else:
                    S.op("vector", lambda h, cf=cf, yf=yf, jc=jc: h.scalar_tensor_tensor(
                        out=yf, in0=cf, scalar=msk[:, jc:jc + 1], in1=yf, op0=ALU.mult, op1=ALU.add),
                        reads=[Bc_, Bmsk, By], writes=[By])
        S.dma("sync", lambda h, xv=xv, t0=t0: h.dma_start(
            out=xv, in_=x_in[t0:t0 + TT, :].rearrange("(j p) d -> p j d", p=128)), Bx, writes=[Bx])
        for j in range(TT // 128):
            ov, Bo = os_[j % 2]
            for half in range(2):
                po, Bpo = C.pbank()
                for kc in range(NKC):
                    S.op("tensor", lambda h, kc=kc, j=j, half=half, po=po, yv=yv: h.matmul(
                        po, lhsT=yv[:, kc, j * 128:(j + 1) * 128], rhs=wo[:, kc, half * 512:(half + 1) * 512],
                        start=(kc == 0), stop=(kc == NKC - 1)), reads=[By, Bwo], writes=[Bpo])
                S.op("vector", lambda h, ov=ov, xv=xv, j=j, half=half, po=po: h.tensor_tensor(
                    out=ov[:, half * 512:(half + 1) * 512], in0=po, in1=xv[:, j, half * 512:(half + 1) * 512],
                    op=ALU.add), reads=[Bpo, Bx], writes=[Bo])
            r0 = t0 + j * 128
            S.dma("sync", lambda h, ov=ov, r0=r0: h.dma_start(out=x_out[r0:r0 + 128, :], in_=ov), Bo, reads=[Bo])
    S.barrier()


def build_ffn_program(NT, emit=False, final=False):
    nc = bass.Bass("TRN2", target_bir_lowering=False)
    x = nc.dram_tensor("x", [NT, D], F32, kind="ExternalInput").ap()
    wg = nc.dram_tensor("wg", [D, DFF], F32, kind="ExternalInput").ap()
    wu = nc.dram_tensor("wu", [D, DFF], F32, kind="ExternalInput").ap()
    wd = nc.dram_tensor("wd", [DFF, D], F32, kind="ExternalInput").ap()
    cst = nc.dram_tensor("cst", [128, 8], F32, kind="ExternalInput").ap()
    ident = nc.dram_tensor("ident", [128, 128], F32, kind="ExternalInput").ap()
    xo = nc.dram_tensor("xo", [NT, D], F32, kind="ExternalOutput").ap()
    xnT = nc.dram_tensor("xnT", [D, NT], BF16, kind="ExternalOutput").ap() if emit else None
    fo = nc.dram_tensor("fo", [NT, D], F32, kind="ExternalOutput").ap() if final else None
    nwf = nc.dram_tensor("nwf", [1, D], F32, kind="ExternalInput").ap() if final else None
    with ExitStack() as st:
        C = Ctx(nc, st)
        C.debug = globals().get("DEBUG", False)
        phase_ffn(C, NT, x, xo, wg, wu, wd, cst, ident, emit_xnT=xnT, final_out=fo, nwf_d=nwf)
        C.S.finish()
        C.S.emit(st)
    return nc


def mixer_host_consts():
    idx = np.arange(64)
    s_, t_ = idx[:, None], idx[None, :]
    mats = np.zeros((64, 6, 64), np.float32)
    mats[:, 0, :] = (s_ <= t_)
    mats[:, 1, :] = np.where(s_ <= t_, 0.0, -30000.0)
    mats[:, 2, :] = np.where(t_ <= s_, 0.0, -30000.0)
    mats[:, 3, :] = (s_ < t_)
    mats[:, 4, :] = (t_ < s_)
    mats[:, 5, :] = np.eye(64)
    seg = np.ones((1, 512), np.float32)
    seg[0, ::64] = 0.0
    return mats, seg


W_OFF = dict(hq=0, hf=256, hi=512, hg=768, gq=1024, gk=1536, gv=2048, gz=2560, gb=3072, ga=3076, rx=3080, rg=3336)


def pack_mixer_inputs(inp, layer, g):
    w_in = np.asarray(inp["w_in"][layer], np.float32)
    O = W_OFF

    def cols(o, n):
        return w_in[:, o:o + n]
    w = np.concatenate([
        cols(O["hq"] + g * 64, 64), cols(O["hf"] + g * 64, 64),
        cols(O["gq"] + g * 128, 128), cols(O["gk"] + g * 128, 128), cols(O["gv"] + g * 128, 128),
        cols(O["rx"] + g * 64, 64), cols(O["rg"] + g * 64, 64),
        np.repeat(cols(O["gb"] + g, 1), 128, axis=1), np.repeat(cols(O["ga"] + g, 1), 128, axis=1),
        cols(O["hi"] + g * 64, 64), cols(O["hg"] + g * 64, 64), cols(O["gz"] + g * 128, 128)], axis=1)
    cst = np.zeros((128, MIX_NCST), np.float32)
    cst[:, 0:8] = np.asarray(inp["norm_mix"][layer], np.float32).reshape(8, 128).T
    hlb = np.asarray(inp["hg_lb"], np.float32)
    cst[0:64, 8] = hlb[0, g * 64:(g + 1) * 64]
    cst[0:64, 9] = hlb[layer, g * 64:(g + 1) * 64]
    cw = np.asarray(inp["gd_conv_w"][layer], np.float32)
    for q, base in enumerate([0, 512, 1024]):
        cst[:, 10 + 4 * q:14 + 4 * q] = cw[:, base + g * 128:base + (g + 1) * 128].T
    cst[:, 22] = np.asarray(inp["gd_a_log"], np.float32)[layer, g]
    cst[:, 23] = np.asarray(inp["gd_dt_bias"], np.float32)[layer, g]
    cst[0:64, 24:28] = np.asarray(inp["rg_conv_w"][layer], np.float32)[:, g * 64:(g + 1) * 64].T
    for c, nm in [(28, "rg_conv_b"), (29, "rg_br"), (30, "rg_bi"), (31, "rg_lambda")]:
        cst[0:64, c] = np.asarray(inp[nm][layer], np.float32)[g * 64:(g + 1) * 64]
    rows = np.concatenate([np.asarray(inp["hg_norm_w"][layer], np.float32),
                           np.asarray(inp["gd_norm_w"][layer], np.float32)]).reshape(1, 192)
    rgw = np.concatenate([np.asarray(inp["rg_wr"][layer, g], np.float32),
                          np.asarray(inp["rg_wi"][layer, g], np.float32)], axis=1)
    return dict(w=np.ascontiguousarray(w), cst=cst, rows=rows, rgw=np.ascontiguousarray(rgw))


def build_mixer_program(L, use_lb, nblk=1):
    nc = bass.Bass("TRN2", target_bir_lowering=False)
    xT = nc.dram_tensor("xT", [nblk * D, L // nblk], BF16, kind="ExternalInput").ap()
    w = nc.dram_tensor("w", [D, MIX_NW], F32, kind="ExternalInput").ap()
    cst = nc.dram_tensor("cst", [128, MIX_NCST], F32, kind="ExternalInput").ap()
    rows = nc.dram_tensor("rows", [1, 192], F32, kind="ExternalInput").ap()
    rgw = nc.dram_tensor("rgw", [64, 128], F32, kind="ExternalInput").ap()
    mats = nc.dram_tensor("mats", [64, 6, 64], F32, kind="ExternalInput").ap()
    seg = nc.dram_tensor("seg", [1, 512], F32, kind="ExternalInput").ap()
    ident = nc.dram_tensor("ident", [128, 128], F32, kind="ExternalInput").ap()
    yT = nc.dram_tensor("yT", [256, L], BF16, kind="ExternalOutput").ap()
    LB = L // nblk

    def xT_src(t0, n):
        r = t0 // LB
        return xT[r * D:(r + 1) * D, t0 - r * LB:t0 - r * LB + n]
    with ExitStack() as st:
        C = Ctx(nc, st)
        C.debug = globals().get("DEBUG", False)
        phase_mixer(C, L, xT_src, w, cst, rows, rgw, mats, seg, ident, yT, use_lb)
        C.S.finish()
        C.S.emit(st)
    return nc


def build_token_program(NT, phases):
    nc = bass.Bass("TRN2", target_bir_lowering=False)
    x = nc.dram_tensor("x", [NT, D], F32, kind="ExternalInput").ap()
    ident = nc.dram_tensor("ident", [128, 128], F32, kind="ExternalInput").ap()
    xo = nc.dram_tensor("xo", [NT, D], F32, kind="ExternalOutput").ap()
    with ExitStack() as st:
        C = Ctx(nc, st)
        cur = x
        for i, ph in enumerate(phases):
            if ph[0] == "wout":
                yT = nc.dram_tensor("yT", [D, NT], BF16, kind="ExternalInput").ap()
                wo = nc.dram_tensor(f"wo{i}", [D, D], F32, kind="ExternalInput").ap()
                phase_wout(C, NT, cur, xo, lambda t0, n, yT=yT: yT[:, t0:t0 + n], wo)
                cur = xo
            else:
                _, emit, final = ph
                wg = nc.dram_tensor(f"wg{i}", [D, DFF], F32, kind="ExternalInput").ap()
                wu = nc.dram_tensor(f"wu{i}", [D, DFF], F32, kind="ExternalInput").ap()
                wd = nc.dram_tensor(f"wd{i}", [DFF, D], F32, kind="ExternalInput").ap()
                cst = nc.dram_tensor(f"cst{i}", [128, 8], F32, kind="ExternalInput").ap()
                xnT = nc.dram_tensor("xnT", [D, NT], BF16, kind="ExternalOutput").ap() if emit else None
                fo = nc.dram_tensor("fo", [NT, D], F32, kind="ExternalOutput").ap() if final else None
                nwf = nc.dram_tensor(f"nwf{i}", [1, D], F32, kind="ExternalInput").ap() if final else None
                phase_ffn(C, NT, cur, xo, wg, wu, wd, cst, ident, emit_xnT=xnT, final_out=fo, nwf_d=nwf)
                cur = xo
        C.S.finish()
        C.S.emit(st)
    return nc


XCH = 512
YCH = 2048
RG4 = [[0, 1, 2, 3], [4, 5, 6, 7]]


def build_fused_program(depth=2, SEQ=SEQ):
    nc = bass.Bass("TRN2", target_bir_lowering=False)
    NT = BATCH * SEQ // NCORES
    XCH_ = min(XCH, NT)
    YCH_ = min(YCH, SEQ)

    def ext(name, shape, dt=F32):
        return nc.dram_tensor(name, shape, dt, kind="ExternalInput").ap()
    x = ext("x", [NT, D])
    ident = ext("ident", [128, 128])
    mats = ext("mats", [64, 6, 64])
    seg = ext("seg", [1, 512])
    nwf = ext("nwf", [1, D])
    msk = ext("msk", [128, 4])
    fo = nc.dram_tensor("fo", [NT, D], F32, kind="ExternalOutput").ap()
    xres = nc.dram_tensor("xres", [NT, D], F32).ap()
    nx, ny = NT // XCH_, SEQ // YCH_
    cinX = [nc.dram_tensor(f"cinX{k}", [D, XCH_], BF16).ap() for k in range(nx)]
    coutX = [nc.dram_tensor(f"coutX{k}", [4 * D, XCH_], BF16).ap() for k in range(nx)]
    cinY = [nc.dram_tensor(f"cinY{k}", [256, YCH_], BF16).ap() for k in range(ny)]
    coutY = [nc.dram_tensor(f"coutY{k}", [4 * 256, YCH_], BF16).ap() for k in range(ny)]
    with ExitStack() as st:
        C = Ctx(nc, st)
        S = C.S
        Bcc = S.buf("cc")

        BcinX = [S.buf("cinX") for _ in range(nx)]
        BcoutX = [S.buf("coutX") for _ in range(nx)]
        BcinY = [S.buf("cinY") for _ in range(ny)]
        BcoutY = [S.buf("coutY") for _ in range(ny)]

        def ag(a, b, Ba, Bb):
            S.dma("gpsimd", lambda h: h.collective_compute(
                "AllGather", ALU.bypass, replica_groups=RG4, ins=[a], outs=[b]), Bcc, reads=[Ba], writes=[Bb], inc=1)

        TTF = min(512, NT)

        def ffn_tile_done(it):
            for k in range(nx):
                if it * TTF < (k + 1) * XCH_ <= (it + 1) * TTF:
                    ag(cinX[k], coutX[k], BcinX[k], BcoutX[k])

        def mix_tile_done(it):
            for k in range(ny):
                if it * 512 < (k + 1) * YCH_ <= (it + 1) * 512:
                    ag(cinY[k], coutY[k], BcinY[k], BcoutY[k])

        def emit_x(r0, n):
            k = r0 // XCH_
            return cinX[k][:, r0 - k * XCH_:r0 - k * XCH_ + n]
        emit_x.dbuf = lambda r0: [BcinX[r0 // XCH_]]

        def xT_src(t0, n):
            r, loc = t0 // NT, t0 % NT
            k = loc // XCH_
            return coutX[k][r * D:(r + 1) * D, loc - k * XCH_:loc - k * XCH_ + n]
        xT_src.dbuf = lambda t0: [BcoutX[(t0 % NT) // XCH_]]

        class YOut:
            def __getitem__(self, key):
                rs, cs = key
                k = cs.start // YCH_
                return cinY[k][rs, cs.start - k * YCH_:cs.stop - k * YCH_]

            def dbuf(self, t0):
                return [BcinY[t0 // YCH_]]

        def ffn_w(tag):
            return (ext(f"wg_{tag}", [D, DFF]), ext(f"wu_{tag}", [D, DFF]), ext(f"wd_{tag}", [DFF, D]), ext(f"cst_{tag}", [128, 8]))
        cur = x
        wg, wu, wd, cst = ffn_w("f1_0")
        phase_ffn(C, NT, cur, xres, wg, wu, wd, cst, ident, emit_xnT=emit_x, on_tile=ffn_tile_done)
        cur = xres
        for layer in range(depth):
            mw = ext(f"mw_{layer}", [D, MIX_NW])
            mc = ext(f"mc_{layer}", [128, MIX_NCST])
            mr = ext(f"mr_{layer}", [1, 192])
            mg = ext(f"mg_{layer}", [64, 128])
            phase_mixer(C, SEQ, xT_src, mw, mc, mr, mg, mats, seg, ident, YOut(), layer > 0, on_tile=mix_tile_done)
            wo = ext(f"wo_{layer}", [D, D])

            def mk(jc):
                def f(t0, n):
                    tg = jc * NT + t0
                    k = tg // YCH_
                    return coutY[k][:, tg - k * YCH_:tg - k * YCH_ + n]
                f.dbuf = lambda t0: [BcoutY[(jc * NT + t0) // YCH_]]
                return f
            phase_wout(C, NT, cur, xres, None, wo, ysrcs=[mk(jc) for jc in range(4)], msk_d=msk)
            last = layer == depth - 1
            wg, wu, wd, cst = ffn_w(f"f2_{layer}")
            if last:
                phase_ffn(C, NT, cur, None, wg, wu, wd, cst, ident, final_out=fo, nwf_d=nwf)
            else:
                phase_ffn(C, NT, cur, xres, wg, wu, wd, cst, ident)
                wg, wu, wd, cst = ffn_w(f"f1_{layer + 1}")
                phase_ffn(C, NT, cur, xres, wg, wu, wd, cst, ident, emit_xnT=emit_x, on_tile=ffn_tile_done)
        C.S.finish()
        C.S.emit(st)
    return nc


WO_PERM = np.concatenate([np.concatenate([np.arange(g * 64, (g + 1) * 64), 256 + np.arange(g * 128, (g + 1) * 128),
                                          768 + np.arange(g * 64, (g + 1) * 64)]) for g in range(4)])

_PROG_CACHE = {}


def _prog(key, fn):
    if key not in _PROG_CACHE:
        _PROG_CACHE[key] = fn()
    return _PROG_CACHE[key]


def _ffn_maps(inp, which, layer, idx):
    f32 = np.float32
    return {f"wg{idx}": np.ascontiguousarray(inp[f"{which}_gate"][layer], f32),
            f"wu{idx}": np.ascontiguousarray(inp[f"{which}_up"][layer], f32),
            f"wd{idx}": np.ascontiguousarray(inp[f"{which}_down"][layer], f32),
            f"cst{idx}": np.ascontiguousarray(np.asarray(inp[f"norm_{which}"][layer], f32).reshape(8, 128).T)}


def kernel_unfused(**inp):
    f32 = np.float32
    depth = inp["w_in"].shape[0]
    x = np.ascontiguousarray(inp["x"], f32).reshape(BATCH * SEQ, D)
    ident = np.eye(128, dtype=f32)
    mats, seg = mixer_host_consts()
    cores = list(range(NCORES))
    xs = [np.ascontiguousarray(x[c * NT_CORE:(c + 1) * NT_CORE]) for c in cores]
    ncA = _prog("A", lambda: build_token_program(NT_CORE, [("ffn", True, False)]))
    wm = _ffn_maps(inp, "ffn1", 0, 0)
    res = run_bass_kernel_spmd(ncA, [dict(x=xs[c], ident=ident, **wm) for c in cores], core_ids=cores).results
    out = None
    for layer in range(depth):
        xs = [np.asarray(res[c]["xo"]) for c in cores]
        xT_b = [np.ascontiguousarray(np.concatenate([np.asarray(res[4 * b + r]["xnT"]) for r in range(4)], axis=0)) for b in range(BATCH)]
        ncB = _prog(("B", layer > 0), lambda: build_mixer_program(SEQ, layer > 0, nblk=4))
        ims = []
        for c in cores:
            d = pack_mixer_inputs(inp, layer, c % 4)
            d.update(xT=xT_b[c // 4], mats=mats, seg=seg, ident=ident)
            ims.append(d)
        resB = run_bass_kernel_spmd(ncB, ims, core_ids=cores).results
        last = layer == depth - 1
        phases = [("wout",), ("ffn", False, last)] + ([] if last else [("ffn", True, False)])
        ncC = _prog(("C", last), lambda: build_token_program(NT_CORE, phases))
        wo = np.ascontiguousarray(np.asarray(inp["w_out"][layer], f32)[WO_PERM, :])
        wm = _ffn_maps(inp, "ffn2", layer, 1)
        if last:
            wm["nwf1"] = np.ascontiguousarray(np.asarray(inp["norm_final"], f32).reshape(1, D))
        else:
            wm.update(_ffn_maps(inp, "ffn1", layer + 1, 2))
        ims = []
        for c in cores:
            b, r = c // 4, c % 4
            yT = np.ascontiguousarray(np.concatenate(
                [np.asarray(resB[4 * b + g]["yT"])[:, r * NT_CORE:(r + 1) * NT_CORE] for g in range(4)], axis=0))
            ims.append(dict(x=xs[c], ident=ident, yT=yT, wo0=wo, **wm))
        res = run_bass_kernel_spmd(ncC, ims, core_ids=cores).results
        if last:
            out = np.concatenate([np.asarray(res[c]["fo"]) for c in cores], axis=0)
    return out.reshape(BATCH, SEQ, D).astype(np.float32)


def kernel_fused(_runner=None, **inp):
    f32 = np.float32
    depth = inp["w_in"].shape[0]
    SEQ = inp["x"].shape[1]
    NT_CORE = BATCH * SEQ // NCORES
    x = np.ascontiguousarray(inp["x"], f32).reshape(BATCH * SEQ, D)
    mats, seg = mixer_host_consts()
    nc = _prog(("F", depth, SEQ), lambda: build_fused_program(depth, SEQ))
    shared = dict(ident=np.eye(128, dtype=f32), mats=mats, seg=seg,
                  nwf=np.ascontiguousarray(np.asarray(inp["norm_final"], f32).reshape(1, D)))

    def ffn(which, layer, tag):
        m = _ffn_maps(inp, which, layer, 0)
        return {f"wg_{tag}": m["wg0"], f"wu_{tag}": m["wu0"], f"wd_{tag}": m["wd0"], f"cst_{tag}": m["cst0"]}
    for layer in range(depth):
        shared.update(ffn("ffn1", layer, f"f1_{layer}"))
        shared.update(ffn("ffn2", layer, f"f2_{layer}"))
        shared[f"wo_{layer}"] = np.ascontiguousarray(np.asarray(inp["w_out"][layer], f32)[WO_PERM, :])
    ims = []
    for c in range(NCORES):
        d = dict(shared)
        d["x"] = np.ascontiguousarray(x[c * NT_CORE:(c + 1) * NT_CORE])
        mk = np.zeros((128, 4), f32)
        mk[:, c % 4] = 1.0
        d["msk"] = mk
        for layer in range(depth):
            p = pack_mixer_inputs(inp, layer, c % 4)
            d[f"mw_{layer}"], d[f"mc_{layer}"], d[f"mr_{layer}"], d[f"mg_{layer}"] = p["w"], p["cst"], p["rows"], p["rgw"]
        ims.append(d)
    if _runner is not None:
        res = _runner(nc, ims)
    else:
        res = run_bass_kernel_spmd(nc, ims, core_ids=list(range(NCORES))).results
    out = np.concatenate([np.asarray(res[c]["fo"]) for c in range(NCORES)], axis=0)
    return out.reshape(BATCH, SEQ, D).astype(np.float32)


def kernel(**inputs):
    return kernel_fused(**inputs)
```

```python
import numpy as np
from contextlib import ExitStack
import concourse.bass as bass
import concourse.mybir as mybir
from concourse.bass_utils import run_bass_kernel_spmd

F32 = mybir.dt.float32
BF16 = mybir.dt.bfloat16
AF = mybir.ActivationFunctionType
ALU = mybir.AluOpType
AX = mybir.AxisListType

D = 1024
DFF = 2816
NFC = DFF // 128
NKC = D // 128
EPS = 1e-6
NCORES = 8
SEQ = 16384
BATCH = 2
NT_CORE = BATCH * SEQ // NCORES

STRICT_SAME_ENGINE = False


class Buf:
    __slots__ = ("name", "w", "r", "dsem", "excl")

    def __init__(self, name, excl=False):
        self.name = name
        self.w = None
        self.r = []
        self.dsem = None
        self.excl = excl


class Eng:
    def __init__(self, name):
        self.name = name
        self.n = 0
        self.seen = {}
        self.ops = []
        self.pending = []


class Sched:
    ENGS = ("tensor", "vector", "scalar", "gpsimd", "sync")

    def __init__(self, nc):
        self.nc = nc
        self.engs = {e: Eng(e) for e in self.ENGS}
        self.dma_count = {}
        self.sem_names = ["E_" + e for e in self.ENGS]
        self.nbuf = 0

    def buf(self, name):
        self.nbuf += 1
        return Buf(f"{name}_{self.nbuf}")

    def _deps(self, E, reads, writes):
        deps = {}
        own = "E_" + E.name
        raw_self = E.name in ("vector", "scalar", "gpsimd")

        def add(tok, is_raw):
            if tok is None:
                return
            k, v = tok
            if k == own and not raw_self and not STRICT_SAME_ENGINE:
                return
            if k.startswith("D_"):
                v = self.dma_count[k]
            if deps.get(k, 0) < v:
                deps[k] = v
        for b in reads:
            add(b.w, True)
        for b in writes:
            add(b.w, False)
            for t in b.r:
                add(t, False)
        out = list(E.pending)
        E.pending = []
        for k, v in deps.items():
            if E.seen.get(k, 0) >= v:
                continue
            E.seen[k] = v
            out.append((k, v))
        return out

    def _finish(self, tok, reads, writes):
        for b in writes:
            b.w = tok
            b.r = []
        for b in reads:
            if b not in writes:
                if len(b.r) > 64:
                    best = {}
                    for k, v in b.r:
                        if best.get(k, 0) < v:
                            best[k] = v
                    b.r = list(best.items())
                b.r.append(tok)

    def op(self, eng, fn, reads=(), writes=()):
        E = self.engs[eng]
        ex = [b for b in reads if b.excl and b not in writes]
        if ex:
            writes = list(writes) + ex
        waits = self._deps(E, reads, writes)
        E.n += 1
        tok = ("E_" + eng, E.n)
        E.ops.append((waits, fn, "E_" + eng, 1))
        self._finish(tok, reads, writes)
        return tok

    def dma(self, eng, fn, sb, reads=(), writes=(), inc=16):
        E = self.engs[eng]
        waits = self._deps(E, reads, writes)
        if sb.dsem is None:
            sb.dsem = "D_" + sb.name
        sem = sb.dsem
        if sem not in self.dma_count:
            self.dma_count[sem] = 0
            self.sem_names.append(sem)
        self.dma_count[sem] += inc
        tok = (sem, self.dma_count[sem])
        E.ops.append((waits, fn, sem, inc))
        self._finish(tok, reads, writes)
        return tok

    def barrier(self):
        for e in self.ENGS:
            E = self.engs[e]
            for k, v in self.dma_count.items():
                if v > 0 and E.seen.get(k, 0) < v:
                    E.pending.append((k, v))
                    E.seen[k] = v
            for e2 in self.ENGS:
                n = self.engs[e2].n
                k = "E_" + e2
                if (e2 != e or e in ('vector', 'scalar', 'gpsimd')) and n > 0 and E.seen.get(k, 0) < n:
                    E.pending.append((k, n))
                    E.seen[k] = n

    def finish(self):
        self.barrier()
        for e in self.ENGS:
            E = self.engs[e]
            if E.pending:
                E.ops.append((E.pending, None, None, 0))
                E.pending = []

    def emit(self, st):
        nc = self.nc
        assert len(self.sem_names) < 140, len(self.sem_names)
        sems = {k: st.enter_context(nc.semaphore(k)) for k in self.sem_names}
        block = st.enter_context(nc.Block())

        def replay(E):
            def body(h):
                for waits, fn, sk, inc in E.ops:
                    for k, v in waits:
                        h.wait_ge(sems[k], v)
                    if fn is not None:
                        fn(h).then_inc(sems[sk], inc)
            return body
        for e in self.ENGS:
            E = self.engs[e]
            if E.ops:
                getattr(block, e)(replay(E))


def _prod(xs):
    r = 1
    for x in xs:
        r *= x
    return r


class Ctx:
    ARENA = 206 * 1024

    def __init__(self, nc, st):
        self.nc = nc
        self.st = st
        self.S = Sched(nc)
        self.arena = st.enter_context(nc.sbuf_tensor("arena", [128, self.ARENA // 2], BF16))
        self.off = 0
        self.psum = [st.enter_context(nc.psum_tensor(f"pb{i}", [128, 512], F32))[:, :] for i in range(8)]
        self.pbuf = [Buf(f"pb{i}", excl=True) for i in range(8)]
        self.pnext = 0
        self.prot = list(range(8))

    def reset(self, off=0):
        self.off = off

    def sb(self, name, shape, dt):
        P = shape[0]
        esz = 4 if dt == F32 else 2
        n = _prod(shape[1:])
        nbytes = (n * esz + 31) // 32 * 32
        assert self.off + nbytes <= self.ARENA, (name, self.off, nbytes)
        v = self.arena[0:P, self.off // 2:(self.off + n * esz) // 2]
        self.off += nbytes
        if dt == F32:
            v = v.bitcast(F32)
        if len(shape) == 3:
            v = v.rearrange("p (a b) -> p a b", b=shape[2])
        elif len(shape) == 4:
            v = v.rearrange("p (a b c) -> p a b c", b=shape[2], c=shape[3])
        return v, self.S.buf(name)

    def dbg(self, name, ap, B):
        if not getattr(self, "debug", False):
            return
        d = self.nc.dram_tensor("dbg_" + name, list(ap.shape), ap.dtype, kind="ExternalOutput").ap()
        self.S.dma("gpsimd", lambda h: h.dma_start(out=d, in_=ap), B, reads=[B])

    def pbank(self):
        i = self.prot[self.pnext % len(self.prot)]
        self.pnext = (self.pnext + 1) % len(self.prot)
        return self.psum[i], self.pbuf[i]


def phase_ffn(C, NT, x_in, x_out, wg_d, wu_d, wd_d, cst_d, ident_d,
              emit_xnT=None, final_out=None, nwf_d=None, on_tile=None):
    S = C.S
    C.reset()
    TT = min(512, NT)
    NJ = TT // 128
    wg, Bwg = C.sb("wg", [128, NKC, DFF], BF16)
    wu, Bwu = C.sb("wu", [128, NKC, DFF], BF16)
    wd, Bwd = C.sb("wd", [128, NFC, D], BF16)
    cst, Bcst = C.sb("cst", [128, 8], F32)
    idb, Bidb = C.sb("idb", [128, 128], BF16)
    base_off = C.off
    idf, Bidf = C.sb("idf", [128, 128], F32)
    S.dma("sync", lambda h: h.dma_start(out=cst, in_=cst_d), Bcst, writes=[Bcst])
    S.dma("sync", lambda h: h.dma_start(out=idf, in_=ident_d), Bidf, writes=[Bidf])
    S.op("vector", lambda h: h.tensor_copy(out=idb, in_=idf), reads=[Bidf], writes=[Bidb])
    NST = 4
    stg = [C.sb(f"stg{i}", [128, DFF], F32) for i in range(NST)]
    jobs = []
    for kc in range(NKC):
        jobs.append((wg_d[kc * 128:(kc + 1) * 128, :], wg[:, kc, :], DFF, cst[:, kc:kc + 1], Bwg))
        jobs.append((wu_d[kc * 128:(kc + 1) * 128, :], wu[:, kc, :], DFF, cst[:, kc:kc + 1], Bwu))
    for fc in range(NFC):
        jobs.append((wd_d[fc * 128:(fc + 1) * 128, :], wd[:, fc, :], D, 0.5, Bwd))
    cengs = ["vector", "scalar"]
    for i, (src, dst, n, sc, Bw) in enumerate(jobs):
        sv, Bs = stg[i % NST]
        deng = "sync" if i % 2 == 0 else "gpsimd"
        S.dma(deng, lambda h, sv=sv, src=src, n=n: h.dma_start(out=sv[:, 0:n], in_=src), Bs, writes=[Bs])
        ce = cengs[i % 2]
        if ce == "scalar":
            S.op("scalar", lambda h, sv=sv, dst=dst, n=n, sc=sc: h.activation(out=dst, in_=sv[:, 0:n], func=AF.Copy, scale=sc),
                 reads=[Bs, Bcst], writes=[Bw])
        else:
            S.op(ce, lambda h, sv=sv, dst=dst, n=n, sc=sc: h.tensor_scalar(out=dst, in0=sv[:, 0:n], scalar1=sc, scalar2=None, op0=ALU.mult),
                 reads=[Bs, Bcst], writes=[Bw])
    C.dbg("wg", wg, Bwg)
    C.dbg("wd", wd, Bwd)
    C.dbg("idb", idb, Bidb)
    S.barrier()
    C.reset(base_off)
    xs = [C.sb(f"xs{i}", [128, D], F32) for i in range(2)]
    xr = [C.sb(f"xr{i}", [128, D], F32) for i in range(2)]
    junk, Bjunk = C.sb("junk", [128, D], BF16)
    xnb = [C.sb(f"xnb{i}", [128, D], BF16) for i in range(2)]
    xnT, BxnT = C.sb("xnT", [128, NKC, TT], BF16)
    hT, BhT = C.sb("hT", [128, NFC, TT], BF16)
    sg = [C.sb(f"sg{i}", [128, TT], F32) for i in range(2)]
    stat = [C.sb(f"stat{i}", [128, 4 * NJ], F32) for i in range(2)]
    if emit_xnT is not None:
        xT2 = [C.sb(f"xT2{i}", [128, NKC, 128], BF16) for i in range(2)]
    if final_out is not None:
        nwf, Bnwf = C.sb("nwf", [128, D], F32)
        S.dma("sync", lambda h: h.dma_start(out=nwf, in_=nwf_d.partition_broadcast(128)), Bnwf, writes=[Bnwf])
        fo1_ = C.sb("fo0", [128, D], F32)
        fo = [fo1_, fo1_]

    def rms_to_bf16T(src_ap, Bsrc, st_ap, Bst, xn_ap, Bxn, dstT, BdstT):
        S.op("gpsimd", lambda h: h.memset(st_ap[:, 0:1], 0.0), writes=[Bst])
        S.op("scalar", lambda h: h.activation(out=junk, in_=src_ap, func=AF.Square, accum_out=st_ap[:, 0:1]),
             reads=[Bsrc], writes=[Bjunk, Bst])
        S.op("scalar", lambda h: h.activation(out=st_ap[:, 1:2], in_=st_ap[:, 0:1], func=AF.Sqrt, scale=1.0 / D, bias=EPS),
             reads=[Bst], writes=[Bst])
        S.op("vector", lambda h: h.reciprocal(out=st_ap[:, 2:3], in_=st_ap[:, 1:2]), reads=[Bst], writes=[Bst])
        S.op("scalar", lambda h: h.activation(out=xn_ap, in_=src_ap, func=AF.Copy, scale=st_ap[:, 2:3]),
             reads=[Bsrc, Bst], writes=[Bxn])
        pt, Bpt = C.pbank()
        ptb = pt.bitcast(BF16).rearrange("p (a b) -> p a b", b=128)
        for kc in range(NKC):
            S.op("tensor", lambda h, kc=kc: h.transpose(out=ptb[:, kc, :], in_=xn_ap[:, kc * 128:(kc + 1) * 128], identity=idb),
                 reads=[Bxn, Bidb], writes=[Bpt])
        S.op("vector", lambda h: h.tensor_copy(out=dstT, in_=ptb), reads=[Bpt], writes=[BdstT])

    nit = NT // TT
    xnTs = [(xnT, BxnT)]
    xnTs.append(C.sb("xnT_b", [128, NKC, TT], BF16))

    def prologue(it):
        t0 = it * TT
        stv, Bst = stat[it % 2]
        xT_, BxT_ = xnTs[it % len(xnTs)]
        for j in range(NJ):
            xv, Bx = xs[j % 2]
            S.dma("sync", lambda h, xv=xv, r0=t0 + j * 128: h.dma_start(out=xv, in_=x_in[r0:r0 + 128, :]), Bx, writes=[Bx])
            xnv, Bxn = xnb[j % 2]
            rms_to_bf16T(xv, Bx, stv[:, 4 * j:4 * j + 4], Bst, xnv, Bxn, xT_[:, :, j * 128:(j + 1) * 128], BxT_)

    prologue(0)
    for it in range(nit):
        t0 = it * TT
        stv, Bst = stat[it % 2]
        xT_, BxT_ = xnTs[it % len(xnTs)]
        for fc in range(NFC):
            pg, Bpg = C.pbank()
            for kc in range(NKC):
                S.op("tensor", lambda h, fc=fc, kc=kc, pg=pg, xT_=xT_: h.matmul(
                    pg, lhsT=wg[:, kc, fc * 128:(fc + 1) * 128], rhs=xT_[:, kc, :],
                    start=(kc == 0), stop=(kc == NKC - 1)), reads=[Bwg, BxT_], writes=[Bpg])
            pu, Bpu = C.pbank()
            for kc in range(NKC):
                S.op("tensor", lambda h, fc=fc, kc=kc, pu=pu, xT_=xT_: h.matmul(
                    pu, lhsT=wu[:, kc, fc * 128:(fc + 1) * 128], rhs=xT_[:, kc, :],
                    start=(kc == 0), stop=(kc == NKC - 1)), reads=[Bwu, BxT_], writes=[Bpu])
            sgv, Bsg = sg[fc % 2]
            S.op("scalar", lambda h, pg=pg, sgv=sgv: h.activation(out=sgv, in_=pg, func=AF.Silu),
                 reads=[Bpg], writes=[Bsg])
            S.op("vector", lambda h, pu=pu, sgv=sgv, fc=fc: h.tensor_tensor(
                out=hT[:, fc, :], in0=sgv, in1=pu, op=ALU.mult), reads=[Bpu, Bsg], writes=[BhT])
        if it + 1 < nit:
            prologue(it + 1)
        for j in range(NJ):
            r0 = t0 + j * 128
            ov, Bo = xr[j % 2]
            S.dma("sync", lambda h, ov=ov, r0=r0: h.dma_start(out=ov, in_=x_in[r0:r0 + 128, :]), Bo, writes=[Bo])
            for half in range(2):
                po, Bpo = C.pbank()
                for fc in range(NFC):
                    S.op("tensor", lambda h, fc=fc, j=j, half=half, po=po: h.matmul(
                        po, lhsT=hT[:, fc, j * 128:(j + 1) * 128], rhs=wd[:, fc, half * 512:(half + 1) * 512],
                        start=(fc == 0), stop=(fc == NFC - 1)), reads=[BhT, Bwd], writes=[Bpo])
                S.op("vector", lambda h, ov=ov, half=half, po=po: h.tensor_tensor(
                    out=ov[:, half * 512:(half + 1) * 512], in0=po, in1=ov[:, half * 512:(half + 1) * 512],
                    op=ALU.add), reads=[Bpo, Bo], writes=[Bo])
            if x_out is not None:
                S.dma("sync", lambda h, ov=ov, r0=r0: h.dma_start(out=x_out[r0:r0 + 128, :], in_=ov), Bo, reads=[Bo])
            if emit_xnT is not None or final_out is not None:
                st2 = stv[:, 4 * j:4 * j + 4]
                xnv, Bxn = xnb[j % 2]
                if emit_xnT is not None:
                    tv, Bt = xT2[j % 2]
                    rms_to_bf16T(ov, Bo, st2, Bst, xnv, Bxn, tv, Bt)
                    S.dma("sync", lambda h, tv=tv, r0=r0: h.dma_start(
                        out=(emit_xnT(r0, 128) if callable(emit_xnT) else emit_xnT[:, r0:r0 + 128]).rearrange("(kc p) t -> p kc t", p=128), in_=tv), Bt, reads=[Bt],
                        writes=(emit_xnT.dbuf(r0) if hasattr(emit_xnT, "dbuf") else []))
                else:
                    S.op("gpsimd", lambda h, st2=st2: h.memset(st2[:, 0:1], 0.0), writes=[Bst])
                    S.op("scalar", lambda h, ov=ov, st2=st2: h.activation(out=junk, in_=ov, func=AF.Square, accum_out=st2[:, 0:1]),
                         reads=[Bo], writes=[Bjunk, Bst])
                    S.op("scalar", lambda h, st2=st2: h.activation(out=st2[:, 1:2], in_=st2[:, 0:1], func=AF.Sqrt, scale=1.0 / D, bias=EPS),
                         reads=[Bst], writes=[Bst])
                    S.op("vector", lambda h, st2=st2: h.reciprocal(out=st2[:, 2:3], in_=st2[:, 1:2]), reads=[Bst], writes=[Bst])
                    fv, Bf = fo[j % 2]
                    S.op("vector", lambda h, ov=ov, st2=st2, fv=fv: h.scalar_tensor_tensor(
                        out=fv, in0=ov, scalar=st2[:, 2:3], in1=nwf, op0=ALU.mult, op1=ALU.mult),
                        reads=[Bo, Bst, Bnwf], writes=[Bf])
                    S.dma("sync", lambda h, fv=fv, r0=r0: h.dma_start(out=final_out[r0:r0 + 128, :], in_=fv), Bf, reads=[Bf])
        if on_tile is not None:
            on_tile(it)
    S.barrier()


MIX_NW = 1152
MIX_NCST = 32
C_HQ, C_HF, C_GQ, C_GK, C_GV, C_RX, C_RG, C_GB, C_GA, C_T = 0, 64, 128, 256, 384, 512, 576, 640, 768, 896
SOLVE_DT = BF16
MIX_STOP = 0


def phase_mixer(C, L, xT_src, w_d, cst_d, rows_d, rgw_d, mats_d, seg_d, ident_d, yT_out, use_lb, on_tile=None):
    S = C.S
    C.reset()
    TM = 512
    NCH = 8
    SD = SOLVE_DT

    def V(fn, r=(), w=()):
        S.op("vector", fn, reads=r, writes=w)

    def G(fn, r=(), w=()):
        S.op("gpsimd", fn, reads=r, writes=w)

    def T(fn, r=(), w=()):
        S.op("tensor", fn, reads=r, writes=w)

    def ACT(out, in_, func, r, w, scale=1.0, bias=0.0):
        S.op("scalar", lambda h: h.activation(out=out, in_=in_, func=func, scale=scale, bias=bias), reads=r, writes=w)

    def c3(ap, j=64):
        return ap.rearrange("p (c j) -> p c j", j=j)

    w, Bw = C.sb("w", [128, NKC, MIX_NW], BF16)
    cst, Bcst = C.sb("cst", [128, MIX_NCST], F32)
    dc, Bdc = C.sb("dc", [128, 8], F32)
    rows, Brows = C.sb("rows", [64, 192], F32)
    rgw, Brgw = C.sb("rgw", [64, 128], F32)
    mats, Bmats = C.sb("mats", [64, 6, 64], F32)
    seg, Bseg = C.sb("seg", [128, TM], F32)
    idf, Bidf = C.sb("idf", [128, 128], F32)
    idb, Bidb = C.sb("idb", [128, 128], BF16)
    onesb, Bonesb = C.sb("onesb", [128, 128], BF16)
    S.dma("sync", lambda h: h.dma_start(out=cst, in_=cst_d), Bcst, writes=[Bcst])
    S.dma("sync", lambda h: h.dma_start(out=rows, in_=rows_d.partition_broadcast(64)), Brows, writes=[Brows])
    S.dma("sync", lambda h: h.dma_start(out=rgw, in_=rgw_d), Brgw, writes=[Brgw])
    S.dma("sync", lambda h: h.dma_start(out=mats, in_=mats_d), Bmats, writes=[Bmats])
    S.dma("sync", lambda h: h.dma_start(out=seg, in_=seg_d.partition_broadcast(128)), Bseg, writes=[Bseg])
    S.dma("sync", lambda h: h.dma_start(out=idf, in_=ident_d), Bidf, writes=[Bidf])
    V(lambda h: h.tensor_copy(out=idb, in_=idf), [Bidf], [Bidb])
    G(lambda h: h.memset(onesb, 1.0), [], [Bonesb])
    if use_lb:
        V(lambda h: h.tensor_tensor(out=dc[:, 4:5], in0=cst[:, 9:10], in1=cst[:, 8:9], op=ALU.subtract), [Bcst], [Bdc])
        ACT(dc[:, 0:1], dc[:, 4:5], AF.Sigmoid, [Bdc], [Bdc])
    else:
        V(lambda h: h.memset(dc[:, 0:1], 0.0), [], [Bdc])
    V(lambda h: h.tensor_scalar(out=dc[:, 1:2], in0=dc[:, 0:1], scalar1=-1.0, scalar2=1.0, op0=ALU.mult, op1=ALU.add), [Bdc], [Bdc])
    ACT(dc[:, 5:6], cst[:, 22:23], AF.Exp, [Bcst], [Bdc])
    V(lambda h: h.tensor_scalar(out=dc[:, 2:3], in0=dc[:, 5:6], scalar1=-1.0, scalar2=None, op0=ALU.mult), [Bdc], [Bdc])
    ACT(dc[:, 6:7], cst[:, 31:32], AF.Exp, [Bcst], [Bdc], scale=-1.0)
    ACT(dc[:, 7:8], dc[:, 6:7], AF.Ln, [Bdc], [Bdc], bias=1.0)
    V(lambda h: h.tensor_scalar(out=dc[:, 3:4], in0=dc[:, 7:8], scalar1=-8.0, scalar2=None, op0=ALU.mult), [Bdc], [Bdc])
    if MIX_STOP == -1:
        return
    base_off = C.off
    stg = [C.sb(f"mstg{i}", [128, MIX_NW], F32) for i in range(2)]
    for kc in range(NKC):
        sv, Bs = stg[kc % 2]
        S.dma("sync" if kc % 2 == 0 else "gpsimd", lambda h, sv=sv, kc=kc: h.dma_start(out=sv, in_=w_d[kc * 128:(kc + 1) * 128, :]), Bs, writes=[Bs])
        if kc % 2 == 0:
            V(lambda h, sv=sv, kc=kc: h.tensor_scalar(out=w[:, kc, :], in0=sv, scalar1=cst[:, kc:kc + 1], scalar2=None, op0=ALU.mult),
              [Bs, Bcst], [Bw])
        else:
            ACT(w[:, kc, :], sv, AF.Copy, [Bs, Bcst], [Bw], scale=cst[:, kc:kc + 1])
    S.barrier()
    C.reset(base_off)

    if MIX_STOP == -2:
        return
    Shg, BShg = C.sb("Shg", [64, 64], F32)
    Shgb, BShgb = C.sb("Shgb", [64, 64], BF16)
    Sgd, BSgd = C.sb("Sgd", [128, 128], F32)
    Sgdb, BSgdb = C.sb("Sgdb", [128, 128], BF16)
    hprev, Bhprev = C.sb("hprev", [64, 1], F32)
    xpq, Bxpq = C.sb("xpq", [128, 3 + TM], F32)
    xpk, Bxpk = C.sb("xpk", [128, 3 + TM], F32)
    xpv, Bxpv = C.sb("xpv", [128, 3 + TM], F32)
    xpr, Bxpr = C.sb("xpr", [64, 3 + TM], F32)
    Am, BAm = C.sb("Am", [64, TM], F32)
    Bm, BBm = C.sb("Bm", [64, TM], F32)
    for ap, B in [(Shg, BShg), (Shgb, BShgb), (Sgd, BSgd), (Sgdb, BSgdb), (hprev, Bhprev)]:
        G(lambda h, ap=ap: h.memset(ap, 0.0), [], [B])
    for ap, B in [(xpq, Bxpq), (xpk, Bxpk), (xpv, Bxpv), (xpr, Bxpr)]:
        G(lambda h, ap=ap: h.memset(ap[:, 0:3], 0.0), [], [B])
    G(lambda h: h.memset(Am[0:32, :], 1.0), [], [BAm])
    G(lambda h: h.memset(Bm[32:64, :], -1.0 / 32), [], [BBm])

    if MIX_STOP == -3:
        return
    xt = [C.sb(f"xt{i}", [128, NKC, TM], BF16) for i in range(2)]

    def f32t(name, P=128, n=TM):
        return C.sb(name, [P, n], F32)

    def b16t(name, P=128, n=TM):
        return C.sb(name, [P, n], BF16)
    qs, Bqs = f32t("qs", 64)
    fg, Bfg = f32t("fg", 64)
    logf, Blogf = f32t("logf", 64)
    kk, Bkk = f32t("kk", 64)
    cum, Bcum = f32t("cum", 64)
    ecum, Becum = f32t("ecum", 64)
    encum, Bencum = f32t("encum", 64)
    edec, Bedec = f32t("edec", 64)
    Qt, BQt = b16t("Qt", 64)
    Kt, BKt = b16t("Kt", 64)
    Kh, BKh = b16t("Kh", 64)
    Khtok, BKhtok = b16t("Khtok", 64)
    scTh, BscTh = b16t("scTh", 64)
    vhg, Bvhg = b16t("vhg", 64)
    gn, Bgn = C.sb("gn", [64, NCH, 192], F32)
    sqo, Bsqo = f32t("sqo", 64, NCH * 128)
    ssn, Bssn = C.sb("ssn", [64, 4 * NCH], F32)
    t1h, Bt1h = f32t("t1h", 64)
    yh, Byh = b16t("yh", 64)
    yTh, ByTh = b16t("yTh", 64)
    cq, Bcq = f32t("cq")
    ck, Bck = f32t("ck")
    cv, Bcv = f32t("cv")
    sqq, Bsqq = b16t("sqq")
    sqk, Bsqk = sqq, Bsqq
    rnq, Brnq = f32t("rnq")
    rnk, Brnk = rnq, Brnq
    qn, Bqn = f32t("qn")
    kn, Bkn = f32t("kn")
    betaB, BbetaB = f32t("betaB")
    gB, BgB = f32t("gB")
    cumB, BcumB = f32t("cumB")
    ecumB, BecumB = f32t("ecumB")
    edecB, BedecB = f32t("edecB")
    qT, BqT = b16t("qT")
    qeT, BqeT = b16t("qeT")
    kT, BkT = b16t("kT")
    kbT, BkbT = b16t("kbT")
    kbeT, BkbeT = b16t("kbeT")
    kdecT, BkdecT = b16t("kdecT")
    vT, BvT = b16t("vT")
    bvT, BbvT = b16t("bvT")
    kbe, Bkbe = C.sb("kbe", [64, NCH, 128], BF16)
    kdec, Bkdec = C.sb("kdec", [64, NCH, 128], BF16)
    bv, Bbv = C.sb("bv", [64, NCH, 128], BF16)
    dtmp, Bdtmp = f32t("dtmp", 64)
    DTm, BDTm = f32t("DTm", 64)
    DTs, BDTs = f32t("DTs", 64)
    Dm, BDm = f32t("Dm", 64)
    Ds, BDs = Dm, BDm
    scT, BscT = b16t("scT", 64)
    Xs = [C.sb(f"Xs{i}", [64, TM], SD) for i in range(2)]
    Ys = [C.sb(f"Ys{i}", [64, TM], SD) for i in range(2)]
    Pm, BPm = C.sb("Pm", [64, TM], SD)
    Pb, BPb = b16t("Pb", 64)
    nwT, BnwT = b16t("nwT")
    vnew = [C.sb(f"vnew{i}", [64, 128], BF16) for i in range(2)]
    t1g, Bt1g = f32t("t1g", 64, NCH * 128)
    yg, Byg = b16t("yg", 64, NCH * 128)
    yTg, ByTg = b16t("yTg")
    xc, Bxc = f32t("xc", 64)
    rr, Brr = f32t("rr", 64)
    ii, Bii = f32t("ii", 64)
    aa, Baa = f32t("aa", 64)
    mm, Bmm = f32t("mm", 64)
    bx, Bbx = f32t("bx", 64)
    hh = [C.sb(f"hh{i}", [64, TM], F32) for i in range(2)]
    gate, Bgate = f32t("gate", 64)
    yTr, ByTr = b16t("yTr", 64)

    save_rot = C.prot
    C.prot = [0, 1, 2, 3, 4]
    C.pnext = 0
    Ohg, BOhg = C.psum[5], C.pbuf[5]
    Og = [(C.psum[6], C.pbuf[6]), (C.psum[7], C.pbuf[7])]

    def fproj(xv, Bx, col, M):
        pb, Bpb = C.pbank()
        for kc in range(NKC):
            T(lambda h, kc=kc, pb=pb: h.matmul(pb[0:M, :], lhsT=w[:, kc, col:col + M], rhs=xv[:, kc, :],
                                               start=(kc == 0), stop=(kc == NKC - 1)), [Bw, Bx], [Bpb])
        return pb, Bpb

    ntiles = L // TM
    cross = {}
    for nm_, (ap_, B_) in dict(scTh=(scTh, BscTh), vhg=(vhg, Bvhg), Qt=(Qt, BQt), Khtok=(Khtok, BKhtok), ecum=(ecum, Becum),
                                bv=(bv, Bbv), qeT=(qeT, BqeT), scT=(scT, BscT), kdec=(kdec, Bkdec), ecumB=(ecumB, BecumB),
                                gn=(gn, Bgn), kbe=(kbe, Bkbe)).items():
        ap2_, B2_ = C.sb(nm_ + "_b", list(ap_.shape), ap_.dtype)
        cross[nm_] = [(ap_, B_), (ap2_, B2_)]
    Xs0 = [Xs[0], C.sb("Xs0b", [64, TM], SD)]
    Ys0 = [Ys[0], C.sb("Ys0b", [64, TM], SD)]
    def tile_A(it):
        scTh, BscTh = cross["scTh"][it % 2]
        vhg, Bvhg = cross["vhg"][it % 2]
        Qt, BQt = cross["Qt"][it % 2]
        Khtok, BKhtok = cross["Khtok"][it % 2]
        ecum, Becum = cross["ecum"][it % 2]
        bv, Bbv = cross["bv"][it % 2]
        qeT, BqeT = cross["qeT"][it % 2]
        scT, BscT = cross["scT"][it % 2]
        kdec, Bkdec = cross["kdec"][it % 2]
        ecumB, BecumB = cross["ecumB"][it % 2]
        gn, Bgn = cross["gn"][it % 2]
        kbe, Bkbe = cross["kbe"][it % 2]
        Xl = [Xs0[it % 2], Xs[1]]
        Yl = [Ys0[it % 2], Ys[1]]
        t0 = it * TM
        yield
        xv, Bx = xt[it % 2]
        S.dma("sync", lambda h, xv=xv, t0=t0: h.dma_start(
            out=xv, in_=xT_src(t0, TM).rearrange("(kc p) t -> p kc t", p=128)), Bx, writes=[Bx],
            reads=(xT_src.dbuf(t0) if hasattr(xT_src, "dbuf") else []))

        yield
        pb, Bpb = fproj(xv, Bx, C_HQ, 64)
        ACT(qs, pb[0:64, :], AF.Silu, [Bpb], [Bqs])
        yield
        pb, Bpb = fproj(xv, Bx, C_HF, 64)
        ACT(fg, pb[0:64, :], AF.Sigmoid, [Bpb], [Bfg])
        yield
        pb, Bpb = fproj(xv, Bx, C_GB, 128)
        ACT(betaB, pb, AF.Sigmoid, [Bpb], [BbetaB])
        yield
        pb, Bpb = fproj(xv, Bx, C_GA, 128)
        ACT(gB, pb, AF.Exp, [Bpb, Bcst], [BgB], bias=cst[:, 23:24])
        yield
        ACT(gB, gB, AF.Ln, [BgB], [BgB], bias=1.0)
        for col, xp, Bxp in [(C_GQ, xpq, Bxpq), (C_GK, xpk, Bxpk), (C_GV, xpv, Bxpv)]:
            pb, Bpb = fproj(xv, Bx, col, 128)
            ACT(xp[:, 3:3 + TM], pb, AF.Copy, [Bpb], [Bxp])
        yield
        pb, Bpb = fproj(xv, Bx, C_RX, 64)
        ACT(xpr[:, 3:3 + TM], pb[0:64, :], AF.Copy, [Bpb], [Bxpr])
        yield
        pb, Bpb = fproj(xv, Bx, C_RG, 64)
        ACT(gate, pb[0:64, :], AF.Gelu_apprx_tanh, [Bpb], [Bgate])
        yield
        for cp in range(NCH // 2):
            pb, Bpb = C.pbank()
            p3 = pb[0:64, :].rearrange("p (c j) -> p c j", j=256)
            for cc in range(2):
                c = cp * 2 + cc
                for kc in range(NKC):
                    T(lambda h, kc=kc, c=c, cc=cc, p3=p3, xv=xv: h.matmul(
                        p3[:, cc, :], lhsT=xv[:, kc, c * 64:(c + 1) * 64], rhs=w[:, kc, C_T:C_T + 256],
                        start=(kc == 0), stop=(kc == NKC - 1)), [Bw, Bx], [Bpb])
            V(lambda h, cp=cp, p3=p3: h.tensor_copy(out=c3(vhg)[:, cp * 2:cp * 2 + 2, :], in_=p3[:, :, 0:64]), [Bpb], [Bvhg])
            for cc in range(2):
                ACT(gn[:, cp * 2 + cc, :], p3[:, cc, 64:256], AF.Silu, [Bpb], [Bgn])
        V(lambda h: h.tensor_tensor(out=gn, in0=gn, in1=rows.unsqueeze(1).to_broadcast([64, NCH, 192]), op=ALU.mult),
          [Bgn, Brows], [Bgn])

        def hg_chain():
            yield
            if use_lb:
                V(lambda h: h.tensor_scalar(out=fg, in0=fg, scalar1=dc[0:64, 1:2], scalar2=dc[0:64, 0:1], op0=ALU.mult, op1=ALU.add),
                  [Bfg, Bdc], [Bfg])
            ACT(logf, fg, AF.Ln, [Bfg], [Blogf])
            yield
            V(lambda h: h.tensor_scalar(out=kk, in0=fg, scalar1=-1.0, scalar2=1.0, op0=ALU.mult, op1=ALU.add), [Bfg], [Bkk])
            V(lambda h: h.tensor_tensor_scan(out=cum, data0=seg[0:64, :], data1=logf, initial=0.0, op0=ALU.mult, op1=ALU.add),
              [Bseg, Blogf], [Bcum])
            yield
            ACT(ecum, cum, AF.Exp, [Bcum], [Becum])
            ACT(encum, cum, AF.Exp, [Bcum], [Bencum], scale=-1.0)
            yield
            V(lambda h: h.tensor_tensor(out=c3(edec), in0=c3(cum)[:, :, 63:64].to_broadcast([64, NCH, 64]), in1=c3(cum), op=ALU.subtract),
              [Bcum], [Bedec])
            ACT(edec, edec, AF.Exp, [Bedec], [Bedec])
            yield
            V(lambda h: h.scalar_tensor_tensor(out=Qt, in0=qs, scalar=0.125, in1=ecum, op0=ALU.mult, op1=ALU.mult), [Bqs, Becum], [BQt])
            V(lambda h: h.tensor_tensor(out=Kt, in0=kk, in1=encum, op=ALU.mult), [Bkk, Bencum], [BKt])
            yield
            V(lambda h: h.tensor_tensor(out=Kh, in0=kk, in1=edec, op=ALU.mult), [Bkk, Bedec], [BKh])
            pb, Bpb = C.pbank()
            yield
            for c in range(NCH):
                T(lambda h, c=c, pb=pb: h.matmul(pb[0:64, c * 64:(c + 1) * 64], lhsT=Kt[:, c * 64:(c + 1) * 64], rhs=Qt[:, c * 64:(c + 1) * 64],
                                                 start=True, stop=True), [BKt, BQt], [Bpb])
            V(lambda h, pb=pb: h.tensor_tensor(out=c3(scTh), in0=c3(pb[0:64, :]), in1=mats[:, 0:1, :].to_broadcast([64, NCH, 64]), op=ALU.mult),
              [Bpb, Bmats], [BscTh])
            yield
            pb, Bpb = C.pbank()
            pbb = pb.bitcast(BF16)
            yield
            for c in range(NCH):
                T(lambda h, c=c, pbb=pbb: h.transpose(out=pbb[0:64, c * 64:(c + 1) * 64], in_=Kh[:, c * 64:(c + 1) * 64], identity=idb[0:64, 0:64]),
                  [BKh, Bidb], [Bpb])
            ACT(Khtok, pbb[0:64, 0:TM], AF.Copy, [Bpb], [BKhtok])

            yield
        def gd_chain():
            yield
            for xp, Bxp, co, Bco, cb in [(xpq, Bxpq, cq, Bcq, 10), (xpk, Bxpk, ck, Bck, 14), (xpv, Bxpv, cv, Bcv, 18)]:
                eng = V
                eng(lambda h, xp=xp, co=co, cb=cb: h.tensor_scalar(out=co, in0=xp[:, 0:TM], scalar1=cst[:, cb:cb + 1], scalar2=None, op0=ALU.mult),
                    [Bxp, Bcst], [Bco])
                for j in range(1, 4):
                    eng(lambda h, xp=xp, co=co, cb=cb, j=j: h.scalar_tensor_tensor(
                        out=co, in0=xp[:, j:j + TM], scalar=cst[:, cb + j:cb + j + 1], in1=co, op0=ALU.mult, op1=ALU.add),
                        [Bxp, Bcst, Bco], [Bco])
                G(lambda h, xp=xp: h.tensor_copy(out=xp[:, 0:3], in_=xp[:, TM:TM + 3]), [Bxp], [Bxp])
            ACT(cq, cq, AF.Silu, [Bcq], [Bcq])
            yield
            ACT(ck, ck, AF.Silu, [Bck], [Bck])
            ACT(vT, cv, AF.Silu, [Bcv], [BvT])
            yield
            for src, Bsrc, sq_, Bsq_, rn, Brn, dst, Bdst, scl in [(cq, Bcq, sqq, Bsqq, rnq, Brnq, qn, Bqn, 128 ** -0.5),
                                                                 (ck, Bck, sqk, Bsqk, rnk, Brnk, kn, Bkn, 1.0)]:
                ACT(sq_, src, AF.Square, [Bsrc], [Bsq_])
                pb, Bpb = C.pbank()
                T(lambda h, pb=pb, sq_=sq_: h.matmul(pb, lhsT=onesb, rhs=sq_, start=True, stop=True), [Bonesb, Bsq_], [Bpb])
                ACT(rn, pb, AF.Sqrt, [Bpb], [Brn], bias=EPS)
                V(lambda h, rn=rn: h.reciprocal(out=rn, in_=rn), [Brn], [Brn])
                V(lambda h, src=src, rn=rn, dst=dst, scl=scl: h.scalar_tensor_tensor(out=dst, in0=src, scalar=scl, in1=rn, op0=ALU.mult, op1=ALU.mult),
                  [Bsrc, Brn], [Bdst])
            V(lambda h: h.tensor_scalar(out=gB, in0=gB, scalar1=dc[:, 2:3], scalar2=None, op0=ALU.mult), [BgB, Bdc], [BgB])
            yield
            V(lambda h: h.tensor_tensor_scan(out=cumB, data0=seg, data1=gB, initial=0.0, op0=ALU.mult, op1=ALU.add), [Bseg, BgB], [BcumB])
            ACT(ecumB, cumB, AF.Exp, [BcumB], [BecumB])
            yield
            V(lambda h: h.tensor_tensor(out=c3(edecB), in0=c3(cumB)[:, :, 63:64].to_broadcast([128, NCH, 64]), in1=c3(cumB), op=ALU.subtract),
              [BcumB], [BedecB])
            ACT(edecB, edecB, AF.Exp, [BedecB], [BedecB])
            yield
            G(lambda h: h.tensor_copy(out=Am[32:64, :], in_=cumB[32:64, :]), [BcumB], [BAm])
            G(lambda h: h.tensor_scalar(out=Bm[0:32, :], in0=cumB[0:32, :], scalar1=1.0 / 32, scalar2=None, op0=ALU.mult), [BcumB], [BBm])
            yield
            ACT(qT, qn, AF.Copy, [Bqn], [BqT])
            V(lambda h: h.tensor_tensor(out=qeT, in0=qn, in1=ecumB, op=ALU.mult), [Bqn, BecumB], [BqeT])
            yield
            ACT(kT, kn, AF.Copy, [Bkn], [BkT])
            V(lambda h: h.tensor_tensor(out=kn, in0=kn, in1=betaB, op=ALU.mult), [Bkn, BbetaB], [Bkn])
            yield
            ACT(kbT, kn, AF.Copy, [Bkn], [BkbT])
            V(lambda h: h.tensor_tensor(out=kbeT, in0=kn, in1=ecumB, op=ALU.mult), [Bkn, BecumB], [BkbeT])
            yield
            V(lambda h: h.tensor_tensor(out=kdecT, in0=kT, in1=edecB, op=ALU.mult), [BkT, BedecB], [BkdecT])
            V(lambda h: h.tensor_tensor(out=bvT, in0=vT, in1=betaB, op=ALU.mult), [BvT, BbetaB], [BbvT])
            yield
            for srcT, BsrcT, dst, Bdst in [(kbeT, BkbeT, kbe, Bkbe), (kdecT, BkdecT, kdec, Bkdec), (bvT, BbvT, bv, Bbv)]:
                pb, Bpb = C.pbank()
                pbb = pb.bitcast(BF16)
                for c in range(NCH):
                    T(lambda h, c=c, pbb=pbb, srcT=srcT: h.transpose(out=pbb[0:64, c * 128:(c + 1) * 128], in_=srcT[:, c * 64:(c + 1) * 64], identity=idb),
                      [BsrcT, Bidb], [Bpb])
                ACT(dst.rearrange("p c j -> p (c j)"), pbb[0:64, :], AF.Copy, [Bpb], [Bdst])
            for lhs_, Blhs, rhs_, Brhs, mi, si, Dfull, BDfull, Dstr, BDstr in [
                    (Am, BAm, Bm, BBm, 1, 3, DTm, BDTm, DTs, BDTs), (Bm, BBm, Am, BAm, 2, 4, Dm, BDm, Ds, BDs)]:
                pb, Bpb = C.pbank()
                for c in range(NCH):
                    T(lambda h, c=c, pb=pb, lhs_=lhs_, rhs_=rhs_: h.matmul(
                        pb[0:64, c * 64:(c + 1) * 64], lhsT=lhs_[:, c * 64:(c + 1) * 64], rhs=rhs_[:, c * 64:(c + 1) * 64],
                        start=True, stop=True), [Blhs, Brhs], [Bpb])
                V(lambda h, pb=pb, mi=mi: h.tensor_tensor(out=c3(dtmp), in0=c3(pb[0:64, :]), in1=mats[:, mi:mi + 1, :].to_broadcast([64, NCH, 64]), op=ALU.add),
                  [Bpb, Bmats], [Bdtmp])
                ACT(Dfull, dtmp, AF.Exp, [Bdtmp], [BDfull])
                V(lambda h, Dfull=Dfull, Dstr=Dstr, si=si: h.tensor_tensor(
                    out=c3(Dstr), in0=c3(Dfull), in1=mats[:, si:si + 1, :].to_broadcast([64, NCH, 64]), op=ALU.mult), [BDfull, Bmats], [BDstr])
            yield
            pb, Bpb = C.pbank()
            for c in range(NCH):
                T(lambda h, c=c, pb=pb: h.matmul(pb[0:64, c * 64:(c + 1) * 64], lhsT=kT[:, c * 64:(c + 1) * 64], rhs=qT[:, c * 64:(c + 1) * 64],
                                                 start=True, stop=True), [BkT, BqT], [Bpb])
            yield
            V(lambda h, pb=pb: h.tensor_tensor(out=scT, in0=pb[0:64, :], in1=DTm, op=ALU.mult), [Bpb, BDTm], [BscT])
            Y0, BY0 = Yl[0]
            yield
            X0, BX0 = Xl[0]
            pb, Bpb = C.pbank()
            yield
            for c in range(NCH):
                T(lambda h, c=c, pb=pb: h.matmul(pb[0:64, c * 64:(c + 1) * 64], lhsT=kT[:, c * 64:(c + 1) * 64], rhs=kbT[:, c * 64:(c + 1) * 64],
                                                 start=True, stop=True), [BkT, BkbT], [Bpb])
            V(lambda h, pb=pb: h.scalar_tensor_tensor(out=Y0, in0=pb[0:64, :], scalar=-1.0, in1=DTs, op0=ALU.mult, op1=ALU.mult), [Bpb, BDTs], [BY0])
            yield
            pb, Bpb = C.pbank()
            for c in range(NCH):
                T(lambda h, c=c, pb=pb: h.matmul(pb[0:64, c * 64:(c + 1) * 64], lhsT=kbT[:, c * 64:(c + 1) * 64], rhs=kT[:, c * 64:(c + 1) * 64],
                                                 start=True, stop=True), [BkT, BkbT], [Bpb])
            yield
            V(lambda h, pb=pb: h.scalar_tensor_tensor(out=X0, in0=pb[0:64, :], scalar=-1.0, in1=Ds, op0=ALU.mult, op1=ALU.mult), [Bpb, BDs], [BX0])
            yield
        def rg_chain():
            V(lambda h: h.tensor_scalar(out=xc, in0=xpr[:, 0:TM], scalar1=cst[0:64, 24:25], scalar2=cst[0:64, 28:29], op0=ALU.mult, op1=ALU.add),
              [Bxpr, Bcst], [Bxc])
            yield
            for j in range(1, 4):
                V(lambda h, j=j: h.scalar_tensor_tensor(out=xc, in0=xpr[:, j:j + TM], scalar=cst[0:64, 24 + j:25 + j], in1=xc, op0=ALU.mult, op1=ALU.add),
                  [Bxpr, Bcst, Bxc], [Bxc])
            G(lambda h: h.tensor_copy(out=xpr[:, 0:3], in_=xpr[:, TM:TM + 3]), [Bxpr], [Bxpr])
            yield
            pb, Bpb = C.pbank()
            T(lambda h, pb=pb: h.matmul(pb[0:64, :], lhsT=rgw[:, 0:64], rhs=xc, start=True, stop=True), [Brgw, Bxc], [Bpb])
            yield
            ACT(rr, pb[0:64, :], AF.Sigmoid, [Bpb, Bcst], [Brr], bias=cst[0:64, 29:30])
            pb, Bpb = C.pbank()
            yield
            T(lambda h, pb=pb: h.matmul(pb[0:64, :], lhsT=rgw[:, 64:128], rhs=xc, start=True, stop=True), [Brgw, Bxc], [Bpb])
            ACT(ii, pb[0:64, :], AF.Sigmoid, [Bpb, Bcst], [Bii], bias=cst[0:64, 30:31])
            yield
            ACT(aa, rr, AF.Exp, [Brr, Bdc], [Baa], scale=dc[0:64, 3:4])
            V(lambda h: h.tensor_tensor(out=mm, in0=aa, in1=aa, op=ALU.mult), [Baa], [Bmm])
            yield
            ACT(mm, mm, AF.Sqrt, [Bmm], [Bmm], scale=-1.0, bias=1.0)
            V(lambda h: h.tensor_tensor(out=bx, in0=ii, in1=xc, op=ALU.mult), [Bii, Bxc], [Bbx])
            yield
            V(lambda h: h.tensor_tensor(out=bx, in0=bx, in1=mm, op=ALU.mult), [Bbx, Bmm], [Bbx])
            hv, Bh = hh[it % 2]
            yield
            V(lambda h, hv=hv: h.tensor_tensor_scan(out=hv, data0=aa, data1=bx, initial=hprev[:, 0:1], op0=ALU.mult, op1=ALU.add),
              [Baa, Bbx, Bhprev], [Bh])
            V(lambda h, hv=hv: h.tensor_copy(out=hprev, in_=hv[:, TM - 1:TM]), [Bh], [Bhprev])
            yield
            V(lambda h, hv=hv: h.tensor_tensor(out=yTr, in0=hv, in1=gate, op=ALU.mult), [Bh, Bgate], [ByTr])
            S.dma("sync", lambda h, t0=t0: h.dma_start(out=yT_out[192:256, t0:t0 + TM], in_=yTr), ByTr, reads=[ByTr],
                  writes=(yT_out.dbuf(t0) if hasattr(yT_out, "dbuf") else []))

            yield
        chains = [gd_chain(), hg_chain()]
        while chains:
            for g_ in list(chains):
                try:
                    next(g_)
                except StopIteration:
                    chains.remove(g_)
                yield
        yield from rg_chain()
        yield

    def tile_B(it):
        t0 = it * TM
        scTh, BscTh = cross["scTh"][it % 2]
        vhg, Bvhg = cross["vhg"][it % 2]
        Qt, BQt = cross["Qt"][it % 2]
        Khtok, BKhtok = cross["Khtok"][it % 2]
        ecum, Becum = cross["ecum"][it % 2]
        bv, Bbv = cross["bv"][it % 2]
        qeT, BqeT = cross["qeT"][it % 2]
        scT, BscT = cross["scT"][it % 2]
        kdec, Bkdec = cross["kdec"][it % 2]
        ecumB, BecumB = cross["ecumB"][it % 2]
        gn, Bgn = cross["gn"][it % 2]
        kbe, Bkbe = cross["kbe"][it % 2]
        Xl = [Xs0[it % 2], Xs[1]]
        Yl = [Ys0[it % 2], Ys[1]]
        def gdB():
            V(lambda h: h.tensor_tensor(out=c3(Pm), in0=c3(Yl[0][0]), in1=mats[:, 5:6, :].to_broadcast([64, NCH, 64]), op=ALU.add), [Yl[0][1], Bmats], [BPm])
            yield
            cur = 0
            for r in range(1, 6):
                yield
                Xc, BXc = Xl[cur]
                Yc, BYc = Yl[cur]
                Xn, BXn = Xl[1 - cur]
                Yn, BYn = Yl[1 - cur]
                pbx, Bpbx = C.pbank()
                for c in range(NCH):
                    sl = slice(c * 64, (c + 1) * 64)
                    T(lambda h, sl=sl, pbx=pbx, Xc=Xc, Yc=Yc: h.matmul(pbx[0:64, sl], lhsT=Yc[:, sl], rhs=Xc[:, sl], start=True, stop=True),
                      [BXc, BYc], [Bpbx])
                ACT(Xn, pbx[0:64, :], AF.Copy, [Bpbx], [BXn])
                if r < 5:
                    pby, Bpby = C.pbank()
                    for c in range(NCH):
                        sl = slice(c * 64, (c + 1) * 64)
                        T(lambda h, sl=sl, pby=pby, Xc=Xc, Yc=Yc: h.matmul(pby[0:64, sl], lhsT=Xc[:, sl], rhs=Yc[:, sl], start=True, stop=True),
                          [BXc, BYc], [Bpby])
                    V(lambda h, pby=pby, Yn=Yn: h.tensor_copy(out=Yn, in_=pby[0:64, :]), [Bpby], [BYn])
                pbp, Bpbp = C.pbank()
                for c in range(NCH):
                    sl = slice(c * 64, (c + 1) * 64)
                    T(lambda h, sl=sl, pbp=pbp, Xn=Xn: h.matmul(pbp[0:64, sl], lhsT=Xn[:, sl], rhs=Pm[:, sl], start=True, stop=True),
                      [BXn, BPm], [Bpbp])
                V(lambda h, pbp=pbp: h.tensor_tensor(out=Pm, in0=Pm, in1=pbp[0:64, :], op=ALU.add), [Bpbp, BPm], [BPm])
                cur = 1 - cur
            yield
            ACT(Pb, Pm, AF.Copy, [BPm], [BPb])
            pb, Bpb = C.pbank()
            yield
            for c in range(NCH):
                T(lambda h, c=c, pb=pb: h.matmul(pb[:, c * 64:(c + 1) * 64], lhsT=kbe[:, c, :], rhs=Pb[:, c * 64:(c + 1) * 64], start=True, stop=True),
                  [Bkbe, BPb], [Bpb])
            ACT(nwT, pb, AF.Copy, [Bpb], [BnwT], scale=-1.0)

            for c in range(NCH):
                sl = slice(c * 64, (c + 1) * 64)
                yield
                pa, Bpa = C.pbank()
                T(lambda h, sl=sl, pa=pa, c=c: h.matmul(pa[0:64, 0:128], lhsT=Pb[:, sl], rhs=bv[:, c, :], start=True, stop=False), [BPb, Bbv], [Bpa])
                T(lambda h, sl=sl, pa=pa: h.matmul(pa[0:64, 0:128], lhsT=nwT[:, sl], rhs=Sgdb, start=False, stop=True), [BnwT, BSgdb], [Bpa])
                vn, Bvn = vnew[c % 2]
                ACT(vn, pa[0:64, 0:128], AF.Copy, [Bpa], [Bvn])
                og, BOg = Og[c // 4]
                osl = slice((c % 4) * 128, (c % 4 + 1) * 128)
                T(lambda h, sl=sl, og=og, osl=osl: h.matmul(og[0:64, osl], lhsT=qeT[:, sl], rhs=Sgdb, start=True, stop=False), [BqeT, BSgdb], [BOg])
                T(lambda h, sl=sl, og=og, osl=osl, vn=vn: h.matmul(og[0:64, osl], lhsT=scT[:, sl], rhs=vn, start=False, stop=True), [BscT, Bvn], [BOg])
                pn, Bpn = C.pbank()
                T(lambda h, pn=pn, vn=vn, c=c: h.matmul(pn[:, 0:128], lhsT=kdec[:, c, :], rhs=vn, start=True, stop=True), [Bkdec, Bvn], [Bpn])
                elg = c3(ecumB)[:, c, 63:64]
                V(lambda h, pn=pn, elg=elg: h.scalar_tensor_tensor(out=Sgdb, in0=Sgd, scalar=elg, in1=pn[:, 0:128], op0=ALU.mult, op1=ALU.add),
                  [BSgd, BecumB, Bpn], [BSgdb])
                V(lambda h, pn=pn, elg=elg: h.scalar_tensor_tensor(out=Sgd, in0=Sgd, scalar=elg, in1=pn[:, 0:128], op0=ALU.mult, op1=ALU.add),
                  [BSgd, BecumB, Bpn], [BSgd])
            for hb in range(2):
                og, BOg = Og[hb]
                seg_ = slice(hb * 512, (hb + 1) * 512)
                ACT(sqo[:, seg_], og[0:64, :], AF.Square, [BOg], [Bsqo])
            yield
            V(lambda h: h.tensor_reduce(out=ssn[:, 2 * NCH:3 * NCH], in_=c3(sqo, 128), axis=AX.X, op=ALU.add), [Bsqo], [Bssn])
            ACT(ssn[:, 3 * NCH:4 * NCH], ssn[:, 2 * NCH:3 * NCH], AF.Sqrt, [Bssn], [Bssn], scale=1.0 / 128, bias=EPS)
            yield
            V(lambda h: h.reciprocal(out=ssn[:, 2 * NCH:3 * NCH], in_=ssn[:, 3 * NCH:4 * NCH]), [Bssn], [Bssn])
            for hb in range(2):
                og, BOg = Og[hb]
                seg_ = slice(hb * 512, (hb + 1) * 512)
                V(lambda h, og=og, seg_=seg_, hb=hb: h.tensor_tensor(
                    out=c3(t1g[:, seg_], 128), in0=c3(og[0:64, :], 128),
                    in1=ssn[:, 2 * NCH + hb * 4:2 * NCH + hb * 4 + 4].unsqueeze(2).to_broadcast([64, 4, 128]), op=ALU.mult),
                    [BOg, Bssn], [Bt1g])
            yield
            V(lambda h: h.tensor_tensor(out=c3(yg, 128), in0=c3(t1g, 128), in1=gn[:, :, 64:192], op=ALU.mult), [Bt1g, Bgn], [Byg])
            pb, Bpb = C.pbank()
            yield
            pbb = pb.bitcast(BF16)
            for c in range(NCH):
                T(lambda h, c=c, pbb=pbb: h.transpose(out=pbb[:, c * 64:(c + 1) * 64], in_=yg[:, c * 128:(c + 1) * 128], identity=idb[0:64, 0:64]),
                  [Byg, Bidb], [Bpb])
            yield
            ACT(yTg, pbb[:, 0:TM], AF.Copy, [Bpb], [ByTg])
            S.dma("sync", lambda h, t0=t0: h.dma_start(out=yT_out[64:192, t0:t0 + TM], in_=yTg), ByTg, reads=[ByTg],
                  writes=(yT_out.dbuf(t0) if hasattr(yT_out, "dbuf") else []))
            yield
            yield
        def hgB():
            for c in range(NCH):
                sl = slice(c * 64, (c + 1) * 64)
                yield
                T(lambda h, sl=sl: h.matmul(Ohg[0:64, sl], lhsT=scTh[:, sl], rhs=vhg[:, sl], start=True, stop=False), [BscTh, Bvhg], [BOhg])
                T(lambda h, sl=sl: h.matmul(Ohg[0:64, sl], lhsT=Qt[:, sl], rhs=Shgb, start=False, stop=True), [BQt, BShgb], [BOhg])
                pb, Bpb = C.pbank()
                T(lambda h, sl=sl, pb=pb: h.matmul(pb[0:64, 0:64], lhsT=Khtok[:, sl], rhs=vhg[:, sl], start=True, stop=True), [BKhtok, Bvhg], [Bpb])
                el = c3(ecum)[:, c, 63:64]
                V(lambda h, pb=pb, el=el: h.scalar_tensor_tensor(out=Shgb, in0=Shg, scalar=el, in1=pb[0:64, 0:64], op0=ALU.mult, op1=ALU.add),
                  [BShg, Becum, Bpb], [BShgb])
                V(lambda h, pb=pb, el=el: h.scalar_tensor_tensor(out=Shg, in0=Shg, scalar=el, in1=pb[0:64, 0:64], op0=ALU.mult, op1=ALU.add),
                  [BShg, Becum, Bpb], [BShg])
                yield
            yield
            ACT(sqo[:, 0:TM], Ohg[0:64, :], AF.Square, [BOhg], [Bsqo])
            V(lambda h: h.tensor_reduce(out=ssn[:, 0:NCH], in_=c3(sqo[:, 0:TM]), axis=AX.X, op=ALU.add), [Bsqo], [Bssn])
            yield
            ACT(ssn[:, NCH:2 * NCH], ssn[:, 0:NCH], AF.Sqrt, [Bssn], [Bssn], scale=1.0 / 64, bias=EPS)
            V(lambda h: h.reciprocal(out=ssn[:, 0:NCH], in_=ssn[:, NCH:2 * NCH]), [Bssn], [Bssn])
            yield
            V(lambda h: h.tensor_tensor(out=c3(t1h), in0=c3(Ohg[0:64, :]), in1=ssn[:, 0:NCH].unsqueeze(2).to_broadcast([64, NCH, 64]), op=ALU.mult),
              [BOhg, Bssn], [Bt1h])
            V(lambda h: h.tensor_tensor(out=c3(yh), in0=c3(t1h), in1=gn[:, :, 0:64], op=ALU.mult), [Bt1h, Bgn], [Byh])
            yield
            pb, Bpb = C.pbank()
            pbb = pb.bitcast(BF16)
            yield
            for c in range(NCH):
                sl = slice(c * 64, (c + 1) * 64)
                T(lambda h, sl=sl, pbb=pbb: h.transpose(out=pbb[0:64, sl], in_=yh[:, sl], identity=idb[0:64, 0:64]), [Byh, Bidb], [Bpb])
            ACT(yTh, pbb[0:64, 0:TM], AF.Copy, [Bpb], [ByTh])
            yield
            S.dma("sync", lambda h, t0=t0: h.dma_start(out=yT_out[0:64, t0:t0 + TM], in_=yTh), ByTh, reads=[ByTh],
                  writes=(yT_out.dbuf(t0) if hasattr(yT_out, "dbuf") else []))
            yield
        chains = [gdB(), hgB()]
        while chains:
            for g_ in list(chains):
                try:
                    next(g_)
                except StopIteration:
                    chains.remove(g_)
                yield

    def interleave(g1, g2):
        gens = [g for g in (g1, g2) if g is not None]
        while gens:
            for g in list(gens):
                try:
                    next(g)
                except StopIteration:
                    gens.remove(g)
    interleave(tile_A(0), None)
    for it in range(ntiles):
        interleave(tile_B(it), tile_A(it + 1) if it + 1 < ntiles else None)
        if on_tile is not None:
            on_tile(it)
    C.prot = save_rot
    C.pnext = 0
    S.barrier()


def phase_wout(C, NT, x_in, x_out, yT_src, wo_d, ysrcs=None, msk_d=None):
    S = C.S
    C.reset()
    TT = 512
    wo, Bwo = C.sb("wo", [128, NKC, D], BF16)
    base_off = C.off
    stg = [C.sb(f"ostg{i}", [128, D], F32) for i in range(2)]
    for kc in range(NKC):
        sv, Bs = stg[kc % 2]
        S.dma("sync" if kc % 2 == 0 else "gpsimd", lambda h, sv=sv, kc=kc: h.dma_start(out=sv, in_=wo_d[kc * 128:(kc + 1) * 128, :]), Bs, writes=[Bs])
        if kc % 2 == 0:
            S.op("vector", lambda h, sv=sv, kc=kc: h.tensor_copy(out=wo[:, kc, :], in_=sv), reads=[Bs], writes=[Bwo])
        else:
            S.op("scalar", lambda h, sv=sv, kc=kc: h.activation(out=wo[:, kc, :], in_=sv, func=AF.Copy), reads=[Bs], writes=[Bwo])
    S.barrier()
    C.reset(base_off)
    yt = [C.sb(f"yt{i}", [128, NKC, TT], BF16) for i in range(2)]
    if ysrcs is not None:
        ycand = [C.sb(f"ycand{i}", [128, NKC, TT], BF16) for i in range(4)]
        msk, Bmsk = C.sb("msk", [128, 4], F32)
        S.dma("sync", lambda h: h.dma_start(out=msk, in_=msk_d), Bmsk, writes=[Bmsk])
    xs = [C.sb(f"oxs{i}", [128, TT // 128, D], F32) for i in range(2)]
    os_ = [C.sb(f"oos{i}", [128, D], F32) for i in range(2)]
    for it in range(NT // TT):
        t0 = it * TT
        yv, By = yt[it % 2]
        xv, Bx = xs[it % 2]
        if ysrcs is None:
            S.dma("sync", lambda h, yv=yv, t0=t0: h.dma_start(
                out=yv, in_=yT_src(t0, TT).rearrange("(kc p) t -> p kc t", p=128)), By, writes=[By])
        else:
            for jc in range(4):
                cv_, Bc_ = ycand[jc]
                S.dma("sync", lambda h, cv_=cv_, jc=jc, t0=t0: h.dma_start(
                    out=cv_, in_=ysrcs[jc](t0, TT).rearrange("(kc p) t -> p kc t", p=128)), Bc_, writes=[Bc_],
                    reads=(ysrcs[jc].dbuf(t0) if hasattr(ysrcs[jc], "dbuf") else []))
            yf = yv.rearrange("p a b -> p (a b)")
            for jc in range(4):
                cv_, Bc_ = ycand[jc]
                cf = cv_.rearrange("p a b -> p (a b)")
                if jc == 0:
                    S.op("vector", lambda h, cf=cf, yf=yf: h.tensor_scalar(out=yf, in0=cf, scalar1=msk[:, 0:1], scalar2=None, op0=ALU.mult),
                         reads=[Bc_, Bmsk], writes=[By])
                else:
                    S.op("vector", lambda h, cf=cf, yf=yf, jc=jc: h.scalar_tensor_tensor(
                        out=yf, in0=cf, scalar=msk[:, jc:jc + 1], in1=yf, op0=ALU.mult, op1=ALU.add),
                        reads=[Bc_, Bmsk, By], writes=[By])
        S.dma("sync", lambda h, xv=xv, t0=t0: h.dma_start(
            out=xv, in_=x_in[t0:t0 + TT, :].rearrange("(j p) d -> p j d", p=128)), Bx, writes=[Bx])
        for j in range(TT // 128):
            ov, Bo = os_[j % 2]
            for half in range(2):
                po, Bpo = C.pbank()
                for kc in range(NKC):
                    S.op("tensor", lambda h, kc=kc, j=j, half=half, po=po, yv=yv: h.matmul(
                        po, lhsT=yv[:, kc, j * 128:(j + 1) * 128], rhs=wo[:, kc, half * 512:(half + 1) * 512],
                        start=(kc == 0), stop=(kc == NKC - 1)), reads=[By, Bwo], writes=[Bpo])
                S.op("vector", lambda h, ov=ov, xv=xv, j=j, half=half, po=po: h.tensor_tensor(
                    out=ov[:, half * 512:(half + 1) * 512], in0=po, in1=xv[:, j, half * 512:(half + 1) * 512],
                    op=ALU.add), reads=[Bpo, Bx], writes=[Bo])
            r0 = t0 + j * 128
            S.dma("sync", lambda h, ov=ov, r0=r0: h.dma_start(out=x_out[r0:r0 + 128, :], in_=ov), Bo, reads=[Bo])
    S.barrier()


def build_ffn_program(NT, emit=False, final=False):
    nc = bass.Bass("TRN2", target_bir_lowering=False)
    x = nc.dram_tensor("x", [NT, D], F32, kind="ExternalInput").ap()
    wg = nc.dram_tensor("wg", [D, DFF], F32, kind="ExternalInput").ap()
    wu = nc.dram_tensor("wu", [D, DFF], F32, kind="ExternalInput").ap()
    wd = nc.dram_tensor("wd", [DFF, D], F32, kind="ExternalInput").ap()
    cst = nc.dram_tensor("cst", [128, 8], F32, kind="ExternalInput").ap()
    ident = nc.dram_tensor("ident", [128, 128], F32, kind="ExternalInput").ap()
    xo = nc.dram_tensor("xo", [NT, D], F32, kind="ExternalOutput").ap()
    xnT = nc.dram_tensor("xnT", [D, NT], BF16, kind="ExternalOutput").ap() if emit else None
    fo = nc.dram_tensor("fo", [NT, D], F32, kind="ExternalOutput").ap() if final else None
    nwf = nc.dram_tensor("nwf", [1, D], F32, kind="ExternalInput").ap() if final else None
    with ExitStack() as st:
        C = Ctx(nc, st)
        C.debug = globals().get("DEBUG", False)
        phase_ffn(C, NT, x, xo, wg, wu, wd, cst, ident, emit_xnT=xnT, final_out=fo, nwf_d=nwf)
        C.S.finish()
        C.S.emit(st)
    return nc


def mixer_host_consts():
    idx = np.arange(64)
    s_, t_ = idx[:, None], idx[None, :]
    mats = np.zeros((64, 6, 64), np.float32)
    mats[:, 0, :] = (s_ <= t_)
    mats[:, 1, :] = np.where(s_ <= t_, 0.0, -30000.0)
    mats[:, 2, :] = np.where(t_ <= s_, 0.0, -30000.0)
    mats[:, 3, :] = (s_ < t_)
    mats[:, 4, :] = (t_ < s_)
    mats[:, 5, :] = np.eye(64)
    seg = np.ones((1, 512), np.float32)
    seg[0, ::64] = 0.0
    return mats, seg


W_OFF = dict(hq=0, hf=256, hi=512, hg=768, gq=1024, gk=1536, gv=2048, gz=2560, gb=3072, ga=3076, rx=3080, rg=3336)


def pack_mixer_inputs(inp, layer, g):
    w_in = np.asarray(inp["w_in"][layer], np.float32)
    O = W_OFF

    def cols(o, n):
        return w_in[:, o:o + n]
    w = np.concatenate([
        cols(O["hq"] + g * 64, 64), cols(O["hf"] + g * 64, 64),
        cols(O["gq"] + g * 128, 128), cols(O["gk"] + g * 128, 128), cols(O["gv"] + g * 128, 128),
        cols(O["rx"] + g * 64, 64), cols(O["rg"] + g * 64, 64),
        np.repeat(cols(O["gb"] + g, 1), 128, axis=1), np.repeat(cols(O["ga"] + g, 1), 128, axis=1),
        cols(O["hi"] + g * 64, 64), cols(O["hg"] + g * 64, 64), cols(O["gz"] + g * 128, 128)], axis=1)
    cst = np.zeros((128, MIX_NCST), np.float32)
    cst[:, 0:8] = np.asarray(inp["norm_mix"][layer], np.float32).reshape(8, 128).T
    hlb = np.asarray(inp["hg_lb"], np.float32)
    cst[0:64, 8] = hlb[0, g * 64:(g + 1) * 64]
    cst[0:64, 9] = hlb[layer, g * 64:(g + 1) * 64]
    cw = np.asarray(inp["gd_conv_w"][layer], np.float32)
    for q, base in enumerate([0, 512, 1024]):
        cst[:, 10 + 4 * q:14 + 4 * q] = cw[:, base + g * 128:base + (g + 1) * 128].T
    cst[:, 22] = np.asarray(inp["gd_a_log"], np.float32)[layer, g]
    cst[:, 23] = np.asarray(inp["gd_dt_bias"], np.float32)[layer, g]
    cst[0:64, 24:28] = np.asarray(inp["rg_conv_w"][layer], np.float32)[:, g * 64:(g + 1) * 64].T
    for c, nm in [(28, "rg_conv_b"), (29, "rg_br"), (30, "rg_bi"), (31, "rg_lambda")]:
        cst[0:64, c] = np.asarray(inp[nm][layer], np.float32)[g * 64:(g + 1) * 64]
    rows = np.concatenate([np.asarray(inp["hg_norm_w"][layer], np.float32),
                           np.asarray(inp["gd_norm_w"][layer], np.float32)]).reshape(1, 192)
    rgw = np.concatenate([np.asarray(inp["rg_wr"][layer, g], np.float32),
                          np.asarray(inp["rg_wi"][layer, g], np.float32)], axis=1)
    return dict(w=np.ascontiguousarray(w), cst=cst, rows=rows, rgw=np.ascontiguousarray(rgw))


def build_mixer_program(L, use_lb, nblk=1):
    nc = bass.Bass("TRN2", target_bir_lowering=False)
    xT = nc.dram_tensor("xT", [nblk * D, L // nblk], BF16, kind="ExternalInput").ap()
    w = nc.dram_tensor("w", [D, MIX_NW], F32, kind="ExternalInput").ap()
    cst = nc.dram_tensor("cst", [128, MIX_NCST], F32, kind="ExternalInput").ap()
    rows = nc.dram_tensor("rows", [1, 192], F32, kind="ExternalInput").ap()
    rgw = nc.dram_tensor("rgw", [64, 128], F32, kind="ExternalInput").ap()
    mats = nc.dram_tensor("mats", [64, 6, 64], F32, kind="ExternalInput").ap()
    seg = nc.dram_tensor("seg", [1, 512], F32, kind="ExternalInput").ap()
    ident = nc.dram_tensor("ident", [128, 128], F32, kind="ExternalInput").ap()
    yT = nc.dram_tensor("yT", [256, L], BF16, kind="ExternalOutput").ap()
    LB = L // nblk

    def xT_src(t0, n):
        r = t0 // LB
        return xT[r * D:(r + 1) * D, t0 - r * LB:t0 - r * LB + n]
    with ExitStack() as st:
        C = Ctx(nc, st)
        C.debug = globals().get("DEBUG", False)
        phase_mixer(C, L, xT_src, w, cst, rows, rgw, mats, seg, ident, yT, use_lb)
        C.S.finish()
        C.S.emit(st)
    return nc


def build_token_program(NT, phases):
    nc = bass.Bass("TRN2", target_bir_lowering=False)
    x = nc.dram_tensor("x", [NT, D], F32, kind="ExternalInput").ap()
    ident = nc.dram_tensor("ident", [128, 128], F32, kind="ExternalInput").ap()
    xo = nc.dram_tensor("xo", [NT, D], F32, kind="ExternalOutput").ap()
    with ExitStack() as st:
        C = Ctx(nc, st)
        cur = x
        for i, ph in enumerate(phases):
            if ph[0] == "wout":
                yT = nc.dram_tensor("yT", [D, NT], BF16, kind="ExternalInput").ap()
                wo = nc.dram_tensor(f"wo{i}", [D, D], F32, kind="ExternalInput").ap()
                phase_wout(C, NT, cur, xo, lambda t0, n, yT=yT: yT[:, t0:t0 + n], wo)
                cur = xo
            else:
                _, emit, final = ph
                wg = nc.dram_tensor(f"wg{i}", [D, DFF], F32, kind="ExternalInput").ap()
                wu = nc.dram_tensor(f"wu{i}", [D, DFF], F32, kind="ExternalInput").ap()
                wd = nc.dram_tensor(f"wd{i}", [DFF, D], F32, kind="ExternalInput").ap()
                cst = nc.dram_tensor(f"cst{i}", [128, 8], F32, kind="ExternalInput").ap()
                xnT = nc.dram_tensor("xnT", [D, NT], BF16, kind="ExternalOutput").ap() if emit else None
                fo = nc.dram_tensor("fo", [NT, D], F32, kind="ExternalOutput").ap() if final else None
                nwf = nc.dram_tensor(f"nwf{i}", [1, D], F32, kind="ExternalInput").ap() if final else None
                phase_ffn(C, NT, cur, xo, wg, wu, wd, cst, ident, emit_xnT=xnT, final_out=fo, nwf_d=nwf)
                cur = xo
        C.S.finish()
        C.S.emit(st)
    return nc


XCH = 512
YCH = 2048
RG4 = [[0, 1, 2, 3], [4, 5, 6, 7]]


def build_fused_program(depth=2, SEQ=SEQ):
    nc = bass.Bass("TRN2", target_bir_lowering=False)
    NT = BATCH * SEQ // NCORES
    XCH_ = min(XCH, NT)
    YCH_ = min(YCH, SEQ)

    def ext(name, shape, dt=F32):
        return nc.dram_tensor(name, shape, dt, kind="ExternalInput").ap()
    x = ext("x", [NT, D])
    ident = ext("ident", [128, 128])
    mats = ext("mats", [64, 6, 64])
    seg = ext("seg", [1, 512])
    nwf = ext("nwf", [1, D])
    msk = ext("msk", [128, 4])
    fo = nc.dram_tensor("fo", [NT, D], F32, kind="ExternalOutput").ap()
    xres = nc.dram_tensor("xres", [NT, D], F32).ap()
    nx, ny = NT // XCH_, SEQ // YCH_
    cinX = [nc.dram_tensor(f"cinX{k}", [D, XCH_], BF16).ap() for k in range(nx)]
    coutX = [nc.dram_tensor(f"coutX{k}", [4 * D, XCH_], BF16).ap() for k in range(nx)]
    cinY = [nc.dram_tensor(f"cinY{k}", [256, YCH_], BF16).ap() for k in range(ny)]
    coutY = [nc.dram_tensor(f"coutY{k}", [4 * 256, YCH_], BF16).ap() for k in range(ny)]
    with ExitStack() as st:
        C = Ctx(nc, st)
        S = C.S
        Bcc = S.buf("cc")

        BcinX = [S.buf("cinX") for _ in range(nx)]
        BcoutX = [S.buf("coutX") for _ in range(nx)]
        BcinY = [S.buf("cinY") for _ in range(ny)]
        BcoutY = [S.buf("coutY") for _ in range(ny)]

        def ag(a, b, Ba, Bb):
            S.dma("gpsimd", lambda h: h.collective_compute(
                "AllGather", ALU.bypass, replica_groups=RG4, ins=[a], outs=[b]), Bcc, reads=[Ba], writes=[Bb], inc=1)

        TTF = min(512, NT)

        def ffn_tile_done(it):
            for k in range(nx):
                if it * TTF < (k + 1) * XCH_ <= (it + 1) * TTF:
                    ag(cinX[k], coutX[k], BcinX[k], BcoutX[k])

        def mix_tile_done(it):
            for k in range(ny):
                if it * 512 < (k + 1) * YCH_ <= (it + 1) * 512:
                    ag(cinY[k], coutY[k], BcinY[k], BcoutY[k])

        def emit_x(r0, n):
            k = r0 // XCH_
            return cinX[k][:, r0 - k * XCH_:r0 - k * XCH_ + n]
        emit_x.dbuf = lambda r0: [BcinX[r0 // XCH_]]

        def xT_src(t0, n):
            r, loc = t0 // NT, t0 % NT
            k = loc // XCH_
            return coutX[k][r * D:(r + 1) * D, loc - k * XCH_:loc - k * XCH_ + n]
        xT_src.dbuf = lambda t0: [BcoutX[(t0 % NT) // XCH_]]

        class YOut:
            def __getitem__(self, key):
                rs, cs = key
                k = cs.start // YCH_
                return cinY[k][rs, cs.start - k * YCH_:cs.stop - k * YCH_]

            def dbuf(self, t0):
                return [BcinY[t0 // YCH_]]

        def ffn_w(tag):
            return (ext(f"wg_{tag}", [D, DFF]), ext(f"wu_{tag}", [D, DFF]), ext(f"wd_{tag}", [DFF, D]), ext(f"cst_{tag}", [128, 8]))
        cur = x
        wg, wu, wd, cst = ffn_w("f1_0")
        phase_ffn(C, NT, cur, xres, wg, wu, wd, cst, ident, emit_xnT=emit_x, on_tile=ffn_tile_done)
        cur = xres
        for layer in range(depth):
            mw = ext(f"mw_{layer}", [D, MIX_NW])
            mc = ext(f"mc_{layer}", [128, MIX_NCST])
            mr = ext(f"mr_{layer}", [1, 192])
            mg = ext(f"mg_{layer}", [64, 128])
            phase_mixer(C, SEQ, xT_src, mw, mc, mr, mg, mats, seg, ident, YOut(), layer > 0, on_tile=mix_tile_done)
            wo = ext(f"wo_{layer}", [D, D])

            def mk(jc):
                def f(t0, n):
                    tg = jc * NT + t0
                    k = tg // YCH_
                    return coutY[k][:, tg - k * YCH_:tg - k * YCH_ + n]
                f.dbuf = lambda t0: [BcoutY[(jc * NT + t0) // YCH_]]
                return f
            phase_wout(C, NT, cur, xres, None, wo, ysrcs=[mk(jc) for jc in range(4)], msk_d=msk)
            last = layer == depth - 1
            wg, wu, wd, cst = ffn_w(f"f2_{layer}")
            if last:
                phase_ffn(C, NT, cur, None, wg, wu, wd, cst, ident, final_out=fo, nwf_d=nwf)
            else:
                phase_ffn(C, NT, cur, xres, wg, wu, wd, cst, ident)
                wg, wu, wd, cst = ffn_w(f"f1_{layer + 1}")
                phase_ffn(C, NT, cur, xres, wg, wu, wd, cst, ident, emit_xnT=emit_x, on_tile=ffn_tile_done)
        C.S.finish()
        C.S.emit(st)
    return nc


WO_PERM = np.concatenate([np.concatenate([np.arange(g * 64, (g + 1) * 64), 256 + np.arange(g * 128, (g + 1) * 128),
                                          768 + np.arange(g * 64, (g + 1) * 64)]) for g in range(4)])

_PROG_CACHE = {}


def _prog(key, fn):
    if key not in _PROG_CACHE:
        _PROG_CACHE[key] = fn()
    return _PROG_CACHE[key]


def _ffn_maps(inp, which, layer, idx):
    f32 = np.float32
    return {f"wg{idx}": np.ascontiguousarray(inp[f"{which}_gate"][layer], f32),
            f"wu{idx}": np.ascontiguousarray(inp[f"{which}_up"][layer], f32),
            f"wd{idx}": np.ascontiguousarray(inp[f"{which}_down"][layer], f32),
            f"cst{idx}": np.ascontiguousarray(np.asarray(inp[f"norm_{which}"][layer], f32).reshape(8, 128).T)}


def kernel_unfused(**inp):
    f32 = np.float32
    depth = inp["w_in"].shape[0]
    x = np.ascontiguousarray(inp["x"], f32).reshape(BATCH * SEQ, D)
    ident = np.eye(128, dtype=f32)
    mats, seg = mixer_host_consts()
    cores = list(range(NCORES))
    xs = [np.ascontiguousarray(x[c * NT_CORE:(c + 1) * NT_CORE]) for c in cores]
    ncA = _prog("A", lambda: build_token_program(NT_CORE, [("ffn", True, False)]))
    wm = _ffn_maps(inp, "ffn1", 0, 0)
    res = run_bass_kernel_spmd(ncA, [dict(x=xs[c], ident=ident, **wm) for c in cores], core_ids=cores).results
    out = None
    for layer in range(depth):
        xs = [np.asarray(res[c]["xo"]) for c in cores]
        xT_b = [np.ascontiguousarray(np.concatenate([np.asarray(res[4 * b + r]["xnT"]) for r in range(4)], axis=0)) for b in range(BATCH)]
        ncB = _prog(("B", layer > 0), lambda: build_mixer_program(SEQ, layer > 0, nblk=4))
        ims = []
        for c in cores:
            d = pack_mixer_inputs(inp, layer, c % 4)
            d.update(xT=xT_b[c // 4], mats=mats, seg=seg, ident=ident)
            ims.append(d)
        resB = run_bass_kernel_spmd(ncB, ims, core_ids=cores).results
        last = layer == depth - 1
        phases = [("wout",), ("ffn", False, last)] + ([] if last else [("ffn", True, False)])
        ncC = _prog(("C", last), lambda: build_token_program(NT_CORE, phases))
        wo = np.ascontiguousarray(np.asarray(inp["w_out"][layer], f32)[WO_PERM, :])
        wm = _ffn_maps(inp, "ffn2", layer, 1)
        if last:
            wm["nwf1"] = np.ascontiguousarray(np.asarray(inp["norm_final"], f32).reshape(1, D))
        else:
            wm.update(_ffn_maps(inp, "ffn1", layer + 1, 2))
        ims = []
        for c in cores:
            b, r = c // 4, c % 4
            yT = np.ascontiguousarray(np.concatenate(
                [np.asarray(resB[4 * b + g]["yT"])[:, r * NT_CORE:(r + 1) * NT_CORE] for g in range(4)], axis=0))
            ims.append(dict(x=xs[c], ident=ident, yT=yT, wo0=wo, **wm))
        res = run_bass_kernel_spmd(ncC, ims, core_ids=cores).results
        if last:
            out = np.concatenate([np.asarray(res[c]["fo"]) for c in cores], axis=0)
    return out.reshape(BATCH, SEQ, D).astype(np.float32)


def kernel_fused(_runner=None, **inp):
    f32 = np.float32
    depth = inp["w_in"].shape[0]
    SEQ = inp["x"].shape[1]
    NT_CORE = BATCH * SEQ // NCORES
    x = np.ascontiguousarray(inp["x"], f32).reshape(BATCH * SEQ, D)
    mats, seg = mixer_host_consts()
    nc = _prog(("F", depth, SEQ), lambda: build_fused_program(depth, SEQ))
    shared = dict(ident=np.eye(128, dtype=f32), mats=mats, seg=seg,
                  nwf=np.ascontiguousarray(np.asarray(inp["norm_final"], f32).reshape(1, D)))

    def ffn(which, layer, tag):
        m = _ffn_maps(inp, which, layer, 0)
        return {f"wg_{tag}": m["wg0"], f"wu_{tag}": m["wu0"], f"wd_{tag}": m["wd0"], f"cst_{tag}": m["cst0"]}
    for layer in range(depth):
        shared.update(ffn("ffn1", layer, f"f1_{layer}"))
        shared.update(ffn("ffn2", layer, f"f2_{layer}"))
        shared[f"wo_{layer}"] = np.ascontiguousarray(np.asarray(inp["w_out"][layer], f32)[WO_PERM, :])
    ims = []
    for c in range(NCORES):
        d = dict(shared)
        d["x"] = np.ascontiguousarray(x[c * NT_CORE:(c + 1) * NT_CORE])
        mk = np.zeros((128, 4), f32)
        mk[:, c % 4] = 1.0
        d["msk"] = mk
        for layer in range(depth):
            p = pack_mixer_inputs(inp, layer, c % 4)
            d[f"mw_{layer}"], d[f"mc_{layer}"], d[f"mr_{layer}"], d[f"mg_{layer}"] = p["w"], p["cst"], p["rows"], p["rgw"]
        ims.append(d)
    if _runner is not None:
        res = _runner(nc, ims)
    else:
        res = run_bass_kernel_spmd(nc, ims, core_ids=list(range(NCORES))).results
    out = np.concatenate([np.asarray(res[c]["fo"]) for c in range(NCORES)], axis=0)
    return out.reshape(BATCH, SEQ, D).astype(np.float32)


def kernel(**inputs):
    return kernel_fused(**inputs)
```
